# Optimizing a Trainium2 kernel written in Bass

```python
import jax, jax.numpy as jnp
from jax import lax
import numpy as np

D_MODEL = 1024
BATCH = 16
SEQ = 4096
DEPTH = 4
DEC_BATCH = 8
DEC_SEQ = 64
PAST_LEN = 1024

CHUNK = 64
D_MIX = D_MODEL
D_CONV = D_MIX // 2
D_RNN = D_MIX - D_CONV
N_RNN_HEADS = 8
RNN_HEAD_DIM = D_RNN // N_RNN_HEADS
CONV_A_WIDTH = 31
CONV_B_WIDTH = 4
RGLRU_C = 8.0
D_FF = -(-8 * D_MODEL // (3 * 256)) * 256
D_IN = 2 * D_CONV + 2 * D_RNN
RMS_EPS = 1e-6
LN_EPS = 1e-5

kernel_name = "hymba_conformer_rglru_stream_step"


def rmsnorm(x, g):
    xf = x.astype(jnp.float32)
    y = xf * lax.rsqrt(jnp.mean(xf * xf, axis=-1, keepdims=True) + RMS_EPS)
    return (y * g.astype(jnp.float32)).astype(x.dtype)


def layernorm(x, g, b):
    xf = x.astype(jnp.float32)
    mu = jnp.mean(xf, axis=-1, keepdims=True)
    xc = xf - mu
    var = jnp.mean(xc * xc, axis=-1, keepdims=True)
    y = xc * lax.rsqrt(var + LN_EPS) * g.astype(jnp.float32) + b.astype(jnp.float32)
    return y.astype(x.dtype)


def causal_dwconv(x_padded, w, b):
    c = x_padded.shape[-1]
    out = lax.conv_general_dilated(
        x_padded, w[:, None, :].astype(x_padded.dtype), window_strides=(1,), padding='VALID',
        dimension_numbers=('NWC', 'WIO', 'NWC'), feature_group_count=c)
    return out + b


def rg_lru(xc, h0, reset, gate_r_w, gate_r_b, gate_i_w, gate_i_b, lam):
    bsz, s, _ = xc.shape
    xh = xc.reshape(bsz, s, N_RNN_HEADS, RNN_HEAD_DIM)
    r_t = jax.nn.sigmoid(jnp.einsum('bshi,hij->bshj', xh, gate_r_w).reshape(bsz, s, D_RNN) + gate_r_b)
    i_t = jax.nn.sigmoid(jnp.einsum('bshi,hij->bshj', xh, gate_i_w).reshape(bsz, s, D_RNN) + gate_i_b)
    log_a = (-RGLRU_C * r_t.astype(jnp.float32)) * jax.nn.softplus(-lam.astype(jnp.float32))
    a = jnp.exp(log_a)
    mult = jnp.sqrt(jnp.maximum(1.0 - jnp.exp(2.0 * log_a), 0.0))
    rs = reset[None, :, None]
    mult = jnp.where(rs, 1.0, mult)
    a = jnp.where(rs, 0.0, a)
    bterm = mult * (i_t * xc).astype(jnp.float32)
    bterm = bterm.at[:, 0].add(a[:, 0] * h0)

    def combine(left, right):
        a1, b1 = left
        a2, b2 = right
        return a1 * a2, a2 * b1 + b2

    _, h = lax.associative_scan(combine, (a, bterm), axis=1)
    return h.astype(xc.dtype), h[:, -1]


def layer(x, buf_a, buf_b, h0, reset, norm1_g, w_in, conv_a_w, conv_a_b, ln_a_g, ln_a_b,
          conv_b_w, conv_b_b, gate_r_w, gate_r_b, gate_i_w, gate_i_b, lam, w_out,
          norm2_g, w_ffn_in, w_ffn_out):
    hn = rmsnorm(x, norm1_g)
    z = jnp.einsum('bsd,de->bse', hn, w_in)
    a_val, a_gate, b_x, b_gate = jnp.split(
        z, [D_CONV, 2 * D_CONV, 2 * D_CONV + D_RNN], axis=-1)
    u = a_val * jax.nn.sigmoid(a_gate)
    ua = jnp.concatenate([buf_a.astype(u.dtype), u], axis=1)
    new_buf_a = ua[:, -(CONV_A_WIDTH - 1):]
    ca = causal_dwconv(ua, conv_a_w, conv_a_b)
    ya = jax.nn.silu(layernorm(ca, ln_a_g, ln_a_b))
    xb = jnp.concatenate([buf_b.astype(b_x.dtype), b_x], axis=1)
    new_buf_b = xb[:, -(CONV_B_WIDTH - 1):]
    xcb = causal_dwconv(xb, conv_b_w, conv_b_b)
    hseq, h_last = rg_lru(xcb, h0, reset, gate_r_w, gate_r_b, gate_i_w, gate_i_b, lam)
    yb = hseq * jax.nn.gelu(b_gate)
    x = x + jnp.einsum('bse,ed->bsd', jnp.concatenate([ya, yb], axis=-1), w_out)
    hf = jnp.einsum('bsd,df->bsf', rmsnorm(x, norm2_g), w_ffn_in)
    g, v = jnp.split(hf, 2, axis=-1)
    x = x + jnp.einsum('bsf,fd->bsd', jax.nn.silu(g) * v, w_ffn_out)
    return x, new_buf_a, new_buf_b, h_last.astype(x.dtype)


def trunk(x, bufs_a, bufs_b, h0s, first_chunk, norm1_g, w_in, conv_a_w, conv_a_b, ln_a_g,
          ln_a_b, conv_b_w, conv_b_b, gate_r_w, gate_r_b, gate_i_w, gate_i_b, rglru_lambda,
          w_out, norm2_g, w_ffn_in, w_ffn_out, final_norm_g):
    s = x.shape[1]
    if first_chunk:
        reset = jnp.arange(s) == 0
    else:
        reset = jnp.zeros((s,), dtype=bool)
    new_a, new_b, new_h = [], [], []
    for l in range(DEPTH):
        x, na, nb, nh = layer(
            x, bufs_a[l], bufs_b[l], h0s[l].astype(jnp.float32), reset, norm1_g[l], w_in[l],
            conv_a_w[l], conv_a_b[l], ln_a_g[l], ln_a_b[l], conv_b_w[l], conv_b_b[l],
            gate_r_w[l], gate_r_b[l], gate_i_w[l], gate_i_b[l], rglru_lambda[l], w_out[l],
            norm2_g[l], w_ffn_in[l], w_ffn_out[l])
        new_a.append(na)
        new_b.append(nb)
        new_h.append(nh)
    y = rmsnorm(x, final_norm_g)
    return y, jnp.stack(new_a), jnp.stack(new_b), jnp.stack(new_h)


def setup_inputs(seed: int = 0) -> dict:
    key = jax.random.key(seed)
    ks = jax.random.split(key, 24)
    f32 = jnp.float32
    nrm = lambda k, shape, scale: jax.random.normal(k, shape, f32) * scale
    u = jax.random.uniform(ks[16], (DEPTH, D_RNN), f32, 0.9, 0.999)
    s_lam = u ** (1.0 / RGLRU_C)
    rglru_lambda = jnp.log(s_lam) - jnp.log1p(-s_lam)
    return {
        "x_prompt": nrm(ks[0], (BATCH, SEQ, D_MODEL), 1.0),
        "x_sample": nrm(ks[1], (DEC_BATCH, DEC_SEQ, D_MODEL), 1.0),
        "state_conv_a": nrm(ks[2], (DEPTH, DEC_BATCH, CONV_A_WIDTH - 1, D_CONV), 0.5),
        "state_conv_b": nrm(ks[3], (DEPTH, DEC_BATCH, CONV_B_WIDTH - 1, D_RNN), 1.0),
        "state_rglru": nrm(ks[4], (DEPTH, DEC_BATCH, D_RNN), 0.5),
        "norm1_g": 1.0 + nrm(ks[5], (DEPTH, D_MODEL), 0.02),
        "w_in": nrm(ks[6], (DEPTH, D_MODEL, D_IN), D_MODEL ** -0.5),
        "conv_a_w": nrm(ks[7], (DEPTH, CONV_A_WIDTH, D_CONV), CONV_A_WIDTH ** -0.5),
        "conv_a_b": nrm(ks[8], (DEPTH, D_CONV), 0.01),
        "ln_a_g": 1.0 + nrm(ks[9], (DEPTH, D_CONV), 0.02),
        "ln_a_b": nrm(ks[10], (DEPTH, D_CONV), 0.01),
        "conv_b_w": nrm(ks[11], (DEPTH, CONV_B_WIDTH, D_RNN), CONV_B_WIDTH ** -0.5),
        "conv_b_b": nrm(ks[12], (DEPTH, D_RNN), 0.01),
        "gate_r_w": nrm(ks[13], (DEPTH, N_RNN_HEADS, RNN_HEAD_DIM, RNN_HEAD_DIM), RNN_HEAD_DIM ** -0.5),
        "gate_r_b": nrm(ks[14], (DEPTH, D_RNN), 0.01),
        "gate_i_w": nrm(ks[15], (DEPTH, N_RNN_HEADS, RNN_HEAD_DIM, RNN_HEAD_DIM), RNN_HEAD_DIM ** -0.5),
        "gate_i_b": nrm(ks[17], (DEPTH, D_RNN), 0.01),
        "rglru_lambda": rglru_lambda,
        "w_out": nrm(ks[18], (DEPTH, D_MIX, D_MODEL), D_MIX ** -0.5),
        "norm2_g": 1.0 + nrm(ks[19], (DEPTH, D_MODEL), 0.02),
        "w_ffn_in": nrm(ks[20], (DEPTH, D_MODEL, 2 * D_FF), D_MODEL ** -0.5),
        "w_ffn_out": nrm(ks[21], (DEPTH, D_FF, D_MODEL), D_FF ** -0.5),
        "final_norm_g": 1.0 + nrm(ks[22], (D_MODEL,), 0.02),
    }


def reference(x_prompt, x_sample, state_conv_a, state_conv_b, state_rglru, norm1_g, w_in,
              conv_a_w, conv_a_b, ln_a_g, ln_a_b, conv_b_w, conv_b_b, gate_r_w, gate_r_b,
              gate_i_w, gate_i_b, rglru_lambda, w_out, norm2_g, w_ffn_in, w_ffn_out,
              final_norm_g):
    assert x_sample.shape[1] <= CHUNK
    weights = (norm1_g, w_in, conv_a_w, conv_a_b, ln_a_g, ln_a_b, conv_b_w, conv_b_b,
               gate_r_w, gate_r_b, gate_i_w, gate_i_b, rglru_lambda, w_out, norm2_g,
               w_ffn_in, w_ffn_out, final_norm_g)
    b = x_prompt.shape[0]
    zero_a = jnp.zeros((DEPTH, b, CONV_A_WIDTH - 1, D_CONV), x_prompt.dtype)
    zero_b = jnp.zeros((DEPTH, b, CONV_B_WIDTH - 1, D_RNN), x_prompt.dtype)
    zero_h = jnp.zeros((DEPTH, b, D_RNN), x_prompt.dtype)
    y_prompt, pa, pb, ph = trunk(x_prompt, zero_a, zero_b, zero_h, True, *weights)
    y_sample, sa, sb, sh = trunk(x_sample, state_conv_a, state_conv_b, state_rglru, False, *weights)
    return (y_prompt, y_sample, pa, pb, ph, sa, sb, sh)
```

```python
import contextlib
import numpy as np
import concourse.bass as bass
import concourse.mybir as mybir
from concourse.bass_utils import run_bass_kernel_spmd

F32 = mybir.dt.float32
BF16 = mybir.dt.bfloat16
AF = mybir.ActivationFunctionType
ALU = mybir.AluOpType

D = 1024
KC = 8
DC = 512
JC = 4
DFF = 2816
FC = 22
FH = 11
CAW = 31
CBW = 4
RMS_EPS = 1e-6
LN_EPS = 1e-5
N_CORES = 8

P_G1, P_G2, P_CAW, P_CAB, P_LNG, P_LNB, P_CBW, P_CBB, P_GRB, P_GIB, P_LAM = (
    0, 8, 16, 140, 144, 148, 152, 168, 172, 176, 180)
NP = 184
Q_HRB, Q_HIB, Q_CL, Q_HCL = 0, 4, 8, 12
NQ = 16
W_IN0, W_OUT0, W_FIN0, W_FOUT0, W_DG0, W_PER_L = 0, 8, 12, 34, 50, 52
NPE = 5
WT_COLS = 2048


class Prog:
    ENGS = ("pe", "act", "dve", "pool", "sp")

    def __init__(self, nc, es):
        self.nc = nc
        self.es = es
        self.streams = {e: [] for e in self.ENGS}
        self.sems = {e: es.enter_context(nc.semaphore("s_" + e)) for e in self.ENGS}
        self.cnt = {e: 0 for e in self.ENGS}
        self.known = {e: {} for e in self.ENGS}
        self.res = {}
        self.chan = {}
        self.semobj = {}
        for e in self.ENGS:
            self.semobj[id(self.sems[e])] = self.sems[e]
        self.own = {e: id(self.sems[e]) for e in self.ENGS}
        self.nwaits = 0
        self.dry = False

    def _waits(self, eng, toks):
        need = {}
        for (sid, val) in toks:
            if need.get(sid, 0) < val:
                need[sid] = val
        for sid, val in need.items():
            if sid == self.own[eng]:
                if eng in ("pe", "sp"):
                    continue
            if self.known[eng].get(sid, 0) >= val:
                continue
            self.known[eng][sid] = val
            sem = self.semobj[sid]
            self.streams[eng].append(("w", sem, val))
            self.nwaits += 1

    def _deps(self, reads, writes):
        toks = []
        for k in reads:
            r = self.res.get(k)
            if r is not None and r[0] is not None:
                toks.append(r[0])
        for k in writes:
            r = self.res.get(k)
            if r is not None:
                if r[0] is not None:
                    toks.append(r[0])
                toks.extend(r[1].values())
        return toks

    def _record(self, reads, writes, tok):
        for k in reads:
            r = self.res.get(k)
            if r is None:
                r = [None, {}]
                self.res[k] = r
            old = r[1].get(tok[0])
            if old is None or old[1] < tok[1]:
                r[1][tok[0]] = tok
        for k in writes:
            self.res[k] = [tok, {}]

    def op(self, eng, fn, reads=(), writes=(), inc=True):
        if self.dry:
            return
        self._waits(eng, self._deps(reads, writes))
        if inc:
            self.cnt[eng] += 1
            tok = (self.own[eng], self.cnt[eng])
            self.streams[eng].append(("i", fn, self.sems[eng]))
        else:
            tok = (self.own[eng], self.cnt[eng] + 1)
            self.streams[eng].append(("n", fn))
        self._record(reads, writes, tok)

    def dma(self, q, chan, out, in_, reads=(), writes=()):
        if self.dry:
            return
        c = self.chan.get(chan)
        if c is None:
            sem = self.es.enter_context(self.nc.semaphore("d_" + str(len(self.chan))))
            self.semobj[id(sem)] = sem
            c = [sem, 0]
            self.chan[chan] = c
        toks = self._deps(reads, writes)
        if c[1] > 0:
            toks.append((id(c[0]), c[1]))
        self._waits(q, toks)
        c[1] += 16
        tok = (id(c[0]), c[1])
        self.streams[q].append(("d", out, in_, c[0]))
        self._record(reads, writes, tok)

    def finish(self, q="sp"):
        toks = [(id(c[0]), c[1]) for c in self.chan.values() if c[1] > 0]
        self._waits(q, toks)

    def emit(self):
        nc = self.nc
        block = self.es.enter_context(nc.Block())
        engmap = {"pe": block.tensor, "act": block.scalar, "dve": block.vector,
                  "pool": block.gpsimd, "sp": block.sync}
        for ename in self.ENGS:
            items = self.streams[ename]

            def body(e, items=items):
                for it in items:
                    k = it[0]
                    if k == "w":
                        e.wait_ge(it[1], it[2])
                    elif k == "i":
                        it[1](e).then_inc(it[2], 1)
                    elif k == "n":
                        it[1](e)
                    else:
                        e.dma_start(out=it[1], in_=it[2]).then_inc(it[3], 16)
            engmap[ename](body)


class Rot:
    def __init__(self, items):
        self.items = items
        self.i = 0

    def __call__(self):
        it = self.items[self.i % len(self.items)]
        self.i += 1
        return it


class Stream:
    pass


def build(cfg):
    L = cfg["L"]
    NT = cfg["NT"]
    T = cfg["T"]
    TS = cfg["TS"]
    SEQ = NT * T
    NW = L * W_PER_L
    NRING = cfg.get("NRING", 7)

    nc = bass.Bass("TRN2", target_bir_lowering=False)
    es = contextlib.ExitStack()

    def din(name, shape, dt=F32):
        return nc.dram_tensor(name, list(shape), dt, kind="ExternalInput").ap()

    def dout(name, shape):
        return nc.dram_tensor(name, list(shape), F32, kind="ExternalOutput").ap()

    xT = din("xT", [D, 2 * SEQ])
    xsT = din("xsT", [D, TS])
    sca_in = din("sca", [128, L, JC, CAW - 1])
    scb_in = din("scb", [128, L, JC, CBW - 1])
    srg_in = din("srg", [128, L, JC])
    par_in = din("par", [128, L * NP + 8])
    w_in = din("w_in", [L, D, 4 * DC])
    w_out = din("w_out", [L, D, D])
    w_fin = din("w_fin", [L, D, 2 * DFF])
    w_fout = din("w_fout", [L, DFF, D])
    gr_in = din("gate_r", [L, 8, 64, 64])
    gi_in = din("gate_i", [L, 8, 64, 64])
    ident_in = din("ident", [128, 128])

    yT = dout("yT", [D, 2 * SEQ])
    ysT = dout("ysT", [D, TS])
    pa_o = dout("pa", [128, L, 2, JC, CAW - 1])
    pb_o = dout("pb", [128, L, 2, JC, CBW - 1])
    ph_o = dout("ph", [128, L, 2, JC])
    sa_o = dout("sa", [128, L, JC, CAW - 1])
    sb_o = dout("sb", [128, L, JC, CBW - 1])
    sh_o = dout("sh", [128, L, JC])

    wsc = nc.dram_tensor("wsc", [NW, 128, WT_COLS], BF16).ap()

    def sb(name, shape, dt=F32):
        return es.enter_context(nc.sbuf_tensor("sb_" + name, list(shape), dt))

    P = Prog(nc, es)

    NXB = 3
    xres = [sb("xres%d" % s, [128, KC, T]) for s in range(NXB)]
    actb = [sb("actb%d" % s, [128, KC, T], BF16) for s in range(2)]
    rstd = [sb("rstd%d" % s, [128, T]) for s in range(2)]
    ust = [sb("ust%d" % s, [128, L, JC, CAW - 1]) for s in range(2)]
    xbst = [sb("xbst%d" % s, [128, L, JC, CBW - 1]) for s in range(2)]
    hst = [sb("hst%d" % s, [128, L, JC]) for s in range(2)]
    u_t = sb("u", [128, JC, CAW - 1 + T])
    ca_t = sb("ca", [128, JC, T])
    cab_t = sb("cab", [128, 2 * JC, T], BF16)
    xb_t = sb("xb", [128, JC, CBW - 1 + T])
    gb_t = sb("gb", [128, JC, T], BF16)
    xcb_t = sb("xcb", [128, 2, T])
    xcbb_t = sb("xcbb", [128, 2, T], BF16)
    P_t = sb("Pt", [128, 2, T])
    Q_t = sb("Qt", [128, 2, T])
    R_t = sb("Rt", [128, 2, T])
    ubf_t = sb("ubf", [128, JC, CAW - 1 + T], BF16)
    ident_t = sb("ident", [128, 128])
    lnt_t = sb("lnt", [128, 2, T])
    sq8 = {"M": sb("sqM", [128, KC, T], BF16), "F": sb("sqF", [128, KC, T], BF16)}
    hh_t = sb("hh", [128, FH, T], BF16)
    sg_t = sb("sg", [128, 2, T])
    ring_t = sb("ring", [128, NRING, WT_COLS], BF16)
    par_t = sb("par", [128, L * NP + 8])
    dpar_t = sb("dpar", [128, L, NQ])
    bdg_t = sb("bdg", [128, L * 8, 128], BF16)
    ones_t = sb("ones", [128, 128], BF16)
    psum = [es.enter_context(nc.psum_tensor("ps%d" % i, [128, 512], F32)) for i in range(8)]

    psM = Rot([(psum[i], ("ps", i)) for i in range(0, 3)])
    psF = Rot([(psum[i], ("ps", i)) for i in range(3, 8)])
    thR = Rot([(lnt_t[:, i, :], ("lnt", i)) for i in range(2)])
    sgR = Rot([(sg_t[:, i, :], ("sg", i)) for i in range(2)])
    ring_state = {"i": 0}

    def pcol(l, c0, n=1):
        return par_t[:, l * NP + c0: l * NP + c0 + n]

    def qcol(l, c0, n=1):
        return dpar_t[:, l, c0:c0 + n]

    def A(eng_out, in_, func, reads, writes, bias=0.0, scale=1.0):
        P.op("act", lambda e: e.activation(out=eng_out, in_=in_, func=func, bias=bias, scale=scale),
             reads, writes)

    def MM(out, lhsT, rhs, start, stop, reads, writes, inc=None):
        P.op("pe", lambda e: e.matmul(out, lhsT, rhs, start=start, stop=stop), reads, writes,
             inc=(stop if inc is None else inc))

    def STT(out, in0, scalar, in1, op0, op1, reads, writes):
        P.op("dve", lambda e: e.scalar_tensor_tensor(out=out, in0=in0, scalar=scalar, in1=in1, op0=op0, op1=op1),
             reads, writes)

    def TT(out, in0, in1, op, reads, writes):
        P.op("dve", lambda e: e.tensor_tensor(out=out, in0=in0, in1=in1, op=op), reads, writes)

    def TS2(out, in0, s1, s2, op0, op1, reads, writes, eng="dve"):
        P.op(eng, lambda e: e.tensor_scalar(out=out, in0=in0, scalar1=s1, scalar2=s2, op0=op0, op1=op1),
             reads, writes)

    P.dma("sp", "par", par_t[:], par_in, writes=[("par",)])
    P.op("dve", lambda e: e.memset(ones_t[:], 1.0), writes=[("ones",)])
    P.dma("sp", "ident", ident_t[:], ident_in, writes=[("ident",)])
    for l in range(L):
        A(qcol(l, Q_CL, 4), pcol(l, P_LAM, 4), AF.Exp, [("par",)], [("dpar", l)], scale=-1.0)
        A(qcol(l, Q_CL, 4), qcol(l, Q_CL, 4), AF.Ln, [("dpar", l)], [("dpar", l)], bias=1.0)
        P.op("act", lambda e, l=l: e.mul(qcol(l, Q_HCL, 4), qcol(l, Q_CL, 4), -4.0), [("dpar", l)], [("dpar", l)])
        P.op("act", lambda e, l=l: e.mul(qcol(l, Q_CL, 4), qcol(l, Q_CL, 4), -8.0), [("dpar", l)], [("dpar", l)])
        P.op("act", lambda e, l=l: e.mul(qcol(l, Q_HRB, 4), pcol(l, P_GRB, 4), 0.5), [("par",)], [("dpar", l)])
        P.op("act", lambda e, l=l: e.mul(qcol(l, Q_HIB, 4), pcol(l, P_GIB, 4), 0.5), [("par",)], [("dpar", l)])
    assert L * 8 * 128 <= KC * T
    xk1 = [("x", 1, k) for k in range(KC)]
    stage_flat = xres[1][:].rearrange("p k t -> p (k t)")
    bd_stage = stage_flat[:, 0:L * 8 * 128].rearrange("p (m q) -> p m q", q=128)
    P.op("dve", lambda e: e.memset(bd_stage, 0.0), writes=xk1)
    for l in range(L):
        for g, gin in enumerate((gr_in, gi_in)):
            v = gin[l].rearrange("(c two) i j -> two i c j", two=2)
            m0 = (l * 2 + g) * 4
            P.dma("sp", ("bd", (l * 2 + g) % 4, 0), bd_stage[0:64, m0:m0 + 4, 0:64], v[0], writes=xk1)
            P.dma("sp", ("bd", (l * 2 + g) % 4, 1), bd_stage[64:128, m0:m0 + 4, 64:128], v[1], writes=xk1)
    P.op("dve", lambda e: e.tensor_copy(out=bdg_t[:], in_=bd_stage), reads=xk1, writes=[("bdg",)])

    def wsrc(l, t):
        if t < W_OUT0:
            i = t - W_IN0
            sec, j = (0, i) if i < 4 else (2, i - 4)
            v = w_in[l].rearrange("(k p) (s j q) -> p k s j q", p=128, s=4, j=4, q=128)
            return v[:, :, sec:sec + 2, j, :], 2048, ("inA" if i < 4 else "inB")
        if t < W_FIN0:
            i = t - W_OUT0
            v = w_out[l].rearrange("(k p) c -> p k c", p=128)
            return v[:, :, 256 * i:256 * i + 256], 2048, "out"
        if t < W_FOUT0:
            f = t - W_FIN0
            v = w_fin[l].rearrange("(k p) (s f q) -> p k s f q", p=128, s=2, f=FC, q=128)
            return v[:, :, :, f, :], 2048, "fin"
        i = t - W_FOUT0
        hf, d = i // 8, i % 8
        v = w_fout[l].rearrange("(h kk p) (d q) -> p h kk d q", h=2, kk=FH, p=128, d=8, q=128)
        return v[:, hf, :, d, :], FH * 128, "fout"

    cv_flat = xres[2][:].rearrange("p k t -> p (k t)")
    assert KC * T >= 4096
    cv_stage = []
    for h in range(2):
        keys = [("x", 2, k) for k in range(KC) if (k * T < (h + 1) * 2048 and (k + 1) * T > h * 2048)]
        cv_stage.append((cv_flat[:, h * 2048:(h + 1) * 2048], keys))
    cv_stage.append((ring_t[:, NRING - 2:NRING, :].rearrange("p a b -> p (a b)").bitcast(F32),
                     [("w", NRING - 2), ("w", NRING - 1)]))
    NCV = len(cv_stage)
    cv_state = {"n": 0, "active": True}

    cv_seq = []
    cv_slot = {}
    cv_state.update({"loaded": 0, "cast": 0})
    LOADAHEAD, CASTAHEAD = 2, 1

    def tile_view(slot, t):
        if t >= W_DG0:
            return ring_t[:, slot, 0:2 * NPE * 128].rearrange("p (a b) -> p a b", b=128)
        if t >= W_FOUT0:
            return ring_t[:, slot, 0:FH * 128].rearrange("p (a b) -> p a b", b=128)
        return ring_t[:, slot, :].rearrange("p (a b) -> p a b", b=256)

    def next_slot():
        slot = ring_state["i"] % (NRING - 4 if cv_state["active"] else NRING)
        ring_state["i"] += 1
        return slot

    def cv_issue_load(i):
        l, t = cv_seq[i]
        if t >= W_DG0:
            return
        src, ncols, kind = wsrc(l, t)
        stg, keys = cv_stage[i % NCV]
        if kind == "fout":
            P.dma("sp", ("pl", i % NCV, 0), stg[:, 0:ncols].rearrange("p (a b) -> p a b", b=128), src, writes=keys)
        elif kind == "out":
            P.dma("sp", ("pl", i % NCV, 0), stg.rearrange("p (a b) -> p a b", b=256), src, writes=keys)
        else:
            dst = stg.rearrange("p (a s b) -> p a s b", s=2, b=128)
            P.dma("sp", ("pl", i % NCV, 0), dst[:, :, 0, :], src[:, :, 0, :], writes=keys)
            P.dma("sp", ("pl", i % NCV, 1), dst[:, :, 1, :], src[:, :, 1, :], writes=keys)

    def cv_issue_cast(i):
        l, t = cv_seq[i]
        slot = NRING - 4 + (i % 2)
        cv_slot[i] = slot
        wkey = ("w", slot)
        if t >= W_DG0:
            ci = t - W_DG0
            for jj in range(2):
                j = 2 * ci + jj
                for k in range(NPE):
                    m = jj * NPE + k
                    TS2(ring_t[:, slot, m * 128:(m + 1) * 128], ident_t[:], pcol(l, P_CAW + j * CAW + k), None,
                        ALU.mult, ALU.bypass, [("ident",), ("par",)], [wkey])
            ncols = 2 * NPE * 128
        else:
            src, ncols, kind = wsrc(l, t)
            stg, keys = cv_stage[i % NCV]
            if kind == "inA":
                sv = stg.rearrange("p (a s b) -> p a s b", s=2, b=128)
                dv = ring_t[:, slot, :].rearrange("p (a s b) -> p a s b", s=2, b=128)
                TS2(dv[:, :, 0, :], sv[:, :, 0, :], 0.5, None, ALU.mult, ALU.bypass, keys, [wkey])
                P.op("dve", lambda e: e.tensor_copy(out=dv[:, :, 1, :], in_=sv[:, :, 1, :]), keys, [wkey])
            else:
                o = ring_t[:, slot, 0:ncols]
                i_ = stg[:, 0:ncols]
                P.op("dve", lambda e: e.tensor_copy(out=o, in_=i_), keys, [wkey])
        P.dma("pool", ("pst", slot), wsc[l * W_PER_L + t][:, 0:ncols], ring_t[:, slot, 0:ncols],
              reads=[wkey], writes=[("wsc", l, t)])

    def wfetch(st, l, t):
        if P.dry:
            if st.convert:
                cv_seq.append((l, t))
            return tile_view(0, t), ("w", 0)
        if st.convert:
            n = cv_state["n"]
            cv_state["n"] += 1
            assert cv_seq[n] == (l, t), (n, cv_seq[n], l, t)
            while cv_state["loaded"] < min(n + LOADAHEAD + 1, len(cv_seq)):
                cv_issue_load(cv_state["loaded"])
                cv_state["loaded"] += 1
            while cv_state["cast"] < min(n + CASTAHEAD + 1, len(cv_seq)):
                cv_issue_cast(cv_state["cast"])
                cv_state["cast"] += 1
            slot = cv_slot[n]
            return tile_view(slot, t), ("w", slot)
        slot = next_slot()
        wkey = ("w", slot)
        ncols = FH * 128 if W_FOUT0 <= t < W_DG0 else (2 * NPE * 128 if t >= W_DG0 else 2048)
        P.dma("sp", ("ring", slot), ring_t[:, slot, 0:ncols], wsc[l * W_PER_L + t][:, 0:ncols],
              reads=[("wsc", l, t)], writes=[wkey])
        return tile_view(slot, t), wkey

    def sq_step(st, which, k):
        Tt, xb = st.T, st.xb
        buf = sq8[which]
        A(buf[:, k, :Tt], xres[xb][:, k, :Tt], AF.Square, [("x", xb, k)], [("sq", which, k)])

    def rms_from_sq(st, which, psR):
        s, Tt = st.s, st.T
        bank, bkey = psR()
        buf = sq8[which]
        for k in range(KC):
            MM(bank[:, :Tt], ones_t[:], buf[:, k, :Tt], k == 0, k == KC - 1, [("sq", which, k), ("ones",)], [bkey])
        A(rstd[s][:, :Tt], bank[:, :Tt], AF.Ln, [bkey], [("rstd", s)], bias=RMS_EPS, scale=1.0 / D)
        A(rstd[s][:, :Tt], rstd[s][:, :Tt], AF.Exp, [("rstd", s)], [("rstd", s)], scale=-0.5)

    def norm_to_act(st, l, gcol):
        s, Tt, xb = st.s, st.T, st.xb
        for k in range(KC):
            STT(actb[s][:, k, :Tt], xres[xb][:, k, :Tt], pcol(l, gcol + k), rstd[s][:, :Tt], ALU.mult, ALU.mult,
                [("x", xb, k), ("rstd", s), ("par",)], [("act", s, k)])

    def x_load(st):
        Tt, xb = st.T, st.xb
        P.dma("pool", ("xl", xb), xres[xb][:, :, :Tt],
              st.xsrc.rearrange("(k p) t -> p k t", p=128), writes=[("x", xb, k) for k in range(KC)])
        st.loaded = True

    def tile_prologue(st):
        s, Tt = st.s, st.T
        if not st.loaded:
            x_load(st)
        if st.first:
            if st.sample:
                P.dma("pool", ("sti", 0), ust[s][:], sca_in, writes=[("ust", s, l) for l in range(L)])
                P.dma("pool", ("sti", 1), xbst[s][:], scb_in, writes=[("xbst", s, l) for l in range(L)])
                P.dma("pool", ("sti", 2), hst[s][:], srg_in, writes=[("hst", s, l) for l in range(L)])
            else:
                P.op("act", lambda e: e.memzero(ust[s][:]), writes=[("ust", s, l) for l in range(L)])
                P.op("act", lambda e: e.memzero(xbst[s][:]), writes=[("xbst", s, l) for l in range(L)])

    def task_M(st, l):
        s, Tt, xb = st.s, st.T, st.xb
        tot = 186.0
        prog = 0.0
        if l == 0:
            tile_prologue(st)
            for k in range(KC):
                sq_step(st, "F", k)
        P.op("act", lambda e: e.copy(u_t[:, :, 0:CAW - 1], ust[s][:, l]), [("ust", s, l)], [("uh",)])
        P.op("act", lambda e: e.copy(ubf_t[:, :, 0:CAW - 1], ust[s][:, l]), [("ust", s, l)], [("ubh",)])
        P.op("act", lambda e: e.copy(xb_t[:, :, 0:CBW - 1], xbst[s][:, l]), [("xbst", s, l)], [("xbh",)])
        rms_from_sq(st, "F", psM)
        prog += 3
        yield prog / tot
        norm_to_act(st, l, P_G1)
        prog += 8
        yield prog / tot

        def pairA(j):
            wv, wk = wfetch(st, l, W_IN0 + j)
            bv, bvk = psM()
            bg, bgk = psM()
            for k in range(KC):
                MM(bv[:, :Tt], wv[:, k, 0:128], actb[s][:, k, :Tt], k == 0, k == KC - 1, [wk, ("act", s, k)], [bvk])
            for k in range(KC):
                MM(bg[:, :Tt], wv[:, k, 128:256], actb[s][:, k, :Tt], k == 0, k == KC - 1, [wk, ("act", s, k)], [bgk])
            th, thk = thR()
            A(th[:, :Tt], bg[:, :Tt], AF.Tanh, [bgk], [thk], scale=0.5)
            STT(u_t[:, j, CAW - 1:CAW - 1 + Tt], th[:, :Tt], 1.0, bv[:, :Tt], ALU.add, ALU.mult,
                [thk, bvk], [("u", j)])

        def pairB(j):
            wv, wk = wfetch(st, l, W_IN0 + 4 + j)
            bx, bxk = psM()
            bt, btk = psM()
            for k in range(KC):
                MM(bx[:, :Tt], wv[:, k, 0:128], actb[s][:, k, :Tt], k == 0, k == KC - 1, [wk, ("act", s, k)], [bxk])
            for k in range(KC):
                MM(bt[:, :Tt], wv[:, k, 128:256], actb[s][:, k, :Tt], k == 0, k == KC - 1, [wk, ("act", s, k)], [btk])
            P.op("act", lambda e: e.copy(xb_t[:, j, CBW - 1:CBW - 1 + Tt], bx[:, :Tt]), [bxk], [("xb", j)])
            A(gb_t[:, j, :Tt], bt[:, :Tt], AF.Gelu_apprx_tanh, [btk], [("gb", j)])

        def convA(js):
            for j in js:
                P.op("act", lambda e, j=j: e.copy(ubf_t[:, j, CAW - 1:CAW - 1 + Tt], u_t[:, j, CAW - 1:CAW - 1 + Tt]),
                     [("u", j)], [("ubf", j)])
            yield 0
            dv, dk = wfetch(st, l, W_DG0 + js[0] // 2)
            for jj, j in enumerate(js):
                bank, bkey = psM()
                for k in range(NPE):
                    MM(bank[:, :Tt], dv[:, jj * NPE + k, :], ubf_t[:, j, k:k + Tt], k == 0, k == NPE - 1,
                       [dk, ("ubf", j), ("ubh",)], [bkey])
                TS2(ca_t[:, j, :Tt], bank[:, :Tt], pcol(l, P_CAB + j), None, ALU.add, ALU.bypass,
                    [bkey, ("par",)], [("ca", j)])
            yield 2
            for k in range(NPE, CAW):
                for j in js:
                    src = u_t[:, j, k:k + Tt]
                    w = pcol(l, P_CAW + j * CAW + k)
                    STT(ca_t[:, j, :Tt], src, w, ca_t[:, j, :Tt], ALU.mult, ALU.add,
                        [("u", j), ("uh",), ("ca", j)], [("ca", j)])
                yield 2

        def chainB(gq):
            js = (2 * gq, 2 * gq + 1)
            for kk in range(CBW):
                for j in js:
                    q = j % 2
                    src = xb_t[:, j, kk:kk + Tt]
                    w = pcol(l, P_CBW + j * CBW + kk)
                    if kk == 0:
                        TS2(xcb_t[:, q, :Tt], src, w, pcol(l, P_CBB + j), ALU.mult, ALU.add,
                            [("xb", j), ("xbh",), ("par",)], [("xcb", q)])
                    else:
                        STT(xcb_t[:, q, :Tt], src, w, xcb_t[:, q, :Tt], ALU.mult, ALU.add,
                            [("xb", j), ("xbh",), ("xcb", q)], [("xcb", q)])
            for j in js:
                q = j % 2
                P.op("dve", lambda e, q=q: e.tensor_copy(out=xcbb_t[:, q, :Tt], in_=xcb_t[:, q, :Tt]),
                     [("xcb", q)], [("xcbb", q)])
            yield 9
            yield 0
            for j in js:
                q = j % 2
                pr, prk = psM()
                pi, pik = psM()
                MM(pr[:, :Tt], bdg_t[:, (l * 2 + 0) * 4 + j, :], xcbb_t[:, q, :Tt], True, True,
                   [("bdg",), ("xcbb", q)], [prk])
                MM(pi[:, :Tt], bdg_t[:, (l * 2 + 1) * 4 + j, :], xcbb_t[:, q, :Tt], True, True,
                   [("bdg",), ("xcbb", q)], [pik])
                A(P_t[:, q, :Tt], pr[:, :Tt], AF.Tanh, [prk, ("dpar", l)], [("P", q)],
                  bias=qcol(l, Q_HRB + j), scale=0.5)
                A(Q_t[:, q, :Tt], pi[:, :Tt], AF.Tanh, [pik, ("dpar", l)], [("Q", q)],
                  bias=qcol(l, Q_HIB + j), scale=0.5)
            yield 0
            for j in js:
                q = j % 2
                A(R_t[:, q, :Tt], P_t[:, q, :Tt], AF.Exp, [("P", q)], [("R", q)],
                  bias=qcol(l, Q_HCL + j), scale=qcol(l, Q_HCL + j))
                A(P_t[:, q, :Tt], P_t[:, q, :Tt], AF.Exp, [("P", q)], [("P", q)],
                  bias=qcol(l, Q_CL + j), scale=qcol(l, Q_CL + j))
            yield 0
            for j in js:
                q = j % 2
                A(P_t[:, q, :Tt], P_t[:, q, :Tt], AF.Relu, [("P", q)], [("P", q)], bias=1.0, scale=-1.0)
            for j in js:
                q = j % 2
                A(P_t[:, q, :Tt], P_t[:, q, :Tt], AF.Ln, [("P", q)], [("P", q)], bias=1e-18)
                A(P_t[:, q, :Tt], P_t[:, q, :Tt], AF.Exp, [("P", q)], [("P", q)], bias=float(np.log(0.5)), scale=0.5)
            yield 0
            yield 0
            for j in js:
                q = j % 2
                STT(Q_t[:, q, :Tt], Q_t[:, q, :Tt], 1.0, xcb_t[:, q, :Tt], ALU.add, ALU.mult,
                    [("Q", q), ("xcb", q)], [("Q", q)])
            if st.reset:
                for j in js:
                    q = j % 2
                    P.op("dve", lambda e, q=q: e.memset(P_t[:, q, 0:1], 0.5), [("P", q)], [("P", q)])
            for j in js:
                q = j % 2
                TT(Q_t[:, q, :Tt], Q_t[:, q, :Tt], P_t[:, q, :Tt], ALU.mult, [("Q", q), ("P", q)], [("Q", q)])
            for j in js:
                q = j % 2
                init = 0.0 if st.reset else hst[s][:, l, j:j + 1]
                P.op("dve", lambda e, q=q, init=init: e.tensor_tensor_scan(
                    out=P_t[:, q, :Tt], data0=R_t[:, q, :Tt], data1=Q_t[:, q, :Tt], initial=init,
                    op0=ALU.mult, op1=ALU.add), [("R", q), ("Q", q), ("hst", s, l)], [("P", q)])
            for j in js:
                q = j % 2
                P.op("act", lambda e, q=q, j=j: e.copy(hst[s][:, l, j:j + 1], P_t[:, q, Tt - 1:Tt]),
                     [("P", q)], [("hst", s, l)])
                TT(actb[s][:, JC + j, :Tt], P_t[:, q, :Tt], gb_t[:, j, :Tt], ALU.mult,
                   [("P", q), ("gb", j)], [("act", s, JC + j)])
            yield 10

        def side0():
            pairB(0); yield 0.5
            pairB(1); yield 0.5
            gb0 = chainB(0)
            yield next(gb0)
            pairA(2); yield 1
            yield next(gb0)
            yield next(gb0)
            pairA(3); yield 1
            yield next(gb0)
            pairB(2); yield 0.5
            yield next(gb0)
            pairB(3); yield 0.5
            yield next(gb0)
            for w_ in gb0:
                yield w_

        def interleave(main, side, every):
            nonlocal prog
            n = 0
            sdone = False
            for wgt in main:
                prog += wgt
                n += 1
                if not sdone and n % every == 0:
                    try:
                        prog += next(side)
                    except StopIteration:
                        sdone = True
                yield prog / tot
            while not sdone:
                try:
                    prog += next(side)
                    yield prog / tot
                except StopIteration:
                    sdone = True

        pairA(0)
        prog += 1
        yield prog / tot
        pairA(1)
        prog += 1
        yield prog / tot
        for fr in interleave(convA((0, 1)), side0(), 2):
            yield fr
        def cab_ops(js):
            for j in js:
                P.op("act", lambda e, j=j: e.copy(cab_t[:, j, :Tt], ca_t[:, j, :Tt]), [("ca", j)], [("cab", j)])
                A(cab_t[:, JC + j, :Tt], ca_t[:, j, :Tt], AF.Square, [("ca", j)], [("cab", JC + j)])

        def side1():
            yield 0
            cab_ops((0, 1))
            for w_ in chainB(1):
                yield w_

        for fr in interleave(convA((2, 3)), side1(), 3):
            yield fr
        P.op("act", lambda e: e.copy(xbst[s][:, l], xb_t[:, :, Tt:Tt + CBW - 1]),
             [("xb", j) for j in range(JC)] + [("xbh",)], [("xbst", s, l)])
        if st.last:
            P.dma("pool", ("sto", 1), st.pb_dst(l), xbst[s][:, l], reads=[("xbst", s, l)])
            P.dma("pool", ("sto", 2), st.ph_dst(l), hst[s][:, l], reads=[("hst", s, l)])
        P.op("act", lambda e: e.copy(ust[s][:, l], u_t[:, :, Tt:Tt + CAW - 1]),
             [("u", j) for j in range(JC)] + [("uh",)], [("ust", s, l)])
        if st.last:
            P.dma("pool", ("sto", 0), st.pa_dst(l), ust[s][:, l], reads=[("ust", s, l)])
        cab_ops((2, 3))
        prog += 4
        yield prog / tot
        s1, s1k = psM()
        s2, s2k = psM()
        for j in range(JC):
            MM(s1[:, :Tt], ones_t[:], cab_t[:, j, :Tt], j == 0, j == JC - 1, [("cab", j), ("ones",)], [s1k])
        for j in range(JC):
            MM(s2[:, :Tt], ones_t[:], cab_t[:, JC + j, :Tt], j == 0, j == JC - 1, [("cab", JC + j), ("ones",)], [s2k])
        l0 = lnt_t[:, 0, :Tt]
        l1 = lnt_t[:, 1, :Tt]
        A(l0, s1[:, :Tt], AF.Square, [s1k], [("lnt", 0)], scale=1.0 / DC)
        STT(l1, s2[:, :Tt], 1.0 / DC, l0, ALU.mult, ALU.subtract, [s2k, ("lnt", 0)], [("lnt", 1)])
        A(l1, l1, AF.Relu, [("lnt", 1)], [("lnt", 1)])
        A(l1, l1, AF.Ln, [("lnt", 1)], [("lnt", 1)], bias=LN_EPS)
        A(l1, l1, AF.Exp, [("lnt", 1)], [("lnt", 1)], scale=-0.5)
        STT(l0, s1[:, :Tt], -1.0 / DC, l1, ALU.mult, ALU.mult, [s1k, ("lnt", 1), ("lnt", 0)], [("lnt", 0)])
        prog += 6
        yield prog / tot
        for j in range(JC):
            TT(ca_t[:, j, :Tt], ca_t[:, j, :Tt], l1, ALU.mult, [("ca", j), ("lnt", 1)], [("ca", j)])
        for j in range(JC):
            TT(ca_t[:, j, :Tt], ca_t[:, j, :Tt], l0, ALU.add, [("ca", j), ("lnt", 0)], [("ca", j)])
        prog += 8
        yield prog / tot
        for j in range(JC):
            A(actb[s][:, j, :Tt], ca_t[:, j, :Tt], AF.Silu, [("ca", j), ("par",)], [("act", s, j)],
              bias=pcol(l, P_LNB + j), scale=pcol(l, P_LNG + j))
        prog += 4
        yield prog / tot
        wv = wk = None
        for d in range(KC):
            if d % 2 == 0:
                wv, wk = wfetch(st, l, W_OUT0 + d // 2)
            half = d % 2
            bank, bkey = psM()
            korder = [4, 5, 6, 7, 0, 1, 2, 3]
            for ki, k in enumerate(korder):
                MM(bank[:, :Tt], wv[:, k, half * 128:(half + 1) * 128], actb[s][:, k, :Tt], ki == 0, ki == KC - 1,
                   [wk, ("act", s, k)], [bkey])
            TT(xres[xb][:, d, :Tt], bank[:, :Tt], xres[xb][:, d, :Tt], ALU.add, [bkey, ("x", xb, d)], [("x", xb, d)])
            sq_step(st, "M", d)
            prog += 1
            yield prog / tot
        yield 1.0

    def task_F(st, l, last_layer):
        s, Tt, xb = st.s, st.T, st.xb
        if last_layer and st.next is not None and not st.convert:
            x_load(st.next)
        tot = 16.0 + 2 * (FH * 16 + 8 * FH) + (9.0 if last_layer else 0.0)
        prog = 0.0
        rms_from_sq(st, "M", psF)
        prog += 8
        yield prog / tot
        norm_to_act(st, l, P_G2)
        prog += 8
        yield prog / tot
        for hf in range(2):
            for fi in range(FH):
                f = hf * FH + fi
                wv, wk = wfetch(st, l, W_FIN0 + f)
                bg, bgk = psF()
                bv, bvk = psF()
                for k in range(KC):
                    MM(bg[:, :Tt], wv[:, k, 0:128], actb[s][:, k, :Tt], k == 0, k == KC - 1, [wk, ("act", s, k)], [bgk])
                for k in range(KC):
                    MM(bv[:, :Tt], wv[:, k, 128:256], actb[s][:, k, :Tt], k == 0, k == KC - 1, [wk, ("act", s, k)], [bvk])
                sg, sgk = sgR()
                A(sg[:, :Tt], bg[:, :Tt], AF.Silu, [bgk], [sgk])
                TT(hh_t[:, fi, :Tt], sg[:, :Tt], bv[:, :Tt], ALU.mult, [sgk, bvk], [("hh", fi)])
                prog += 16
                yield prog / tot
            for d in range(KC):
                wv, wk = wfetch(st, l, W_FOUT0 + hf * 8 + d)
                bank, bkey = psF()
                for kk in range(FH):
                    MM(bank[:, :Tt], wv[:, kk, :], hh_t[:, kk, :Tt], kk == 0, kk == FH - 1, [wk, ("hh", kk)], [bkey])
                TT(xres[xb][:, d, :Tt], bank[:, :Tt], xres[xb][:, d, :Tt], ALU.add, [bkey, ("x", xb, d)], [("x", xb, d)])
                if hf == 1:
                    sq_step(st, "F", d)
                prog += FH
                yield prog / tot
        if last_layer:
            rms_from_sq(st, "F", psF)
            for k in range(KC):
                STT(xres[xb][:, k, :Tt], xres[xb][:, k, :Tt], par_t[:, L * NP + k:L * NP + k + 1], rstd[s][:, :Tt],
                    ALU.mult, ALU.mult, [("x", xb, k), ("rstd", s), ("par",)], [("x", xb, k)])
            P.dma("pool", ("ys", xb), st.ydst.rearrange("(k p) t -> p k t", p=128), xres[xb][:, :, :Tt],
                  reads=[("x", xb, k) for k in range(KC)])
            prog += 9
            yield prog / tot
        if last_layer and st.convert:
            cv_state["active"] = False
        yield 1.0

    def mk_tile(s, ti, sample=False):
        st = Stream()
        st.s = s
        st.sample = sample
        st.loaded = False
        st.next = None
        st.convert = False
        if sample:
            st.T = TS
            st.first = True
            st.last = True
            st.reset = False
            st.xsrc = xsT
            st.ydst = ysT
            st.pa_dst = lambda l: sa_o[:, l]
            st.pb_dst = lambda l: sb_o[:, l]
            st.ph_dst = lambda l: sh_o[:, l]
        else:
            st.T = T
            st.first = (ti == 0)
            st.last = (ti == NT - 1)
            st.reset = (ti == 0)
            c0 = s * SEQ + ti * T
            st.xsrc = xT[:, c0:c0 + T]
            st.ydst = yT[:, c0:c0 + T]
            st.pa_dst = lambda l: pa_o[:, l, s]
            st.pb_dst = lambda l: pb_o[:, l, s]
            st.ph_dst = lambda l: ph_o[:, l, s]
        return st

    def tasks_of(st):
        out = []
        for l in range(L):
            out.append(task_M(st, l))
            out.append(task_F(st, l, l == L - 1))
        return out

    listA = []
    listB = []
    for ti in range(NT):
        listA.append(mk_tile(0, ti))
        listB.append(mk_tile(1, ti))
    if cfg.get("SAMPLE", True):
        listA.append(mk_tile(0, 0, sample=True))

    order = []
    for ti in range(max(len(listA), len(listB))):
        if ti < len(listA):
            order.append(listA[ti])
        if ti < len(listB):
            order.append(listB[ti])
    for n, st in enumerate(order):
        st.xb = n % NXB
    listA[0].convert = True
    dummy = mk_tile(0, 0)
    dummy.xb = 0
    dummy.convert = True
    P.dry = True
    for l in range(L):
        for _ in task_M(dummy, l):
            pass
        for _ in task_F(dummy, l, l == L - 1):
            pass
    P.dry = False
    cv_state["active"] = True
    for tl in (listA, listB):
        for a, b in zip(tl[:-1], tl[1:]):
            a.next = b

    F_LEAD = cfg.get("F_LEAD", 0.03)

    def stream_gen(tl, offset):
        idx = 0
        for st in tl:
            for l in range(L):
                for fr in task_M(st, l):
                    yield offset + idx + min(fr, 1.0)
                idx += 1
                for fr in task_F(st, l, l == L - 1):
                    yield offset + idx + min(fr, 1.0) - F_LEAD
                idx += 1

    gA = stream_gen(listA, 0.0)
    gB = stream_gen(listB, 1.0)
    pA = next(gA, None)
    pB = next(gB, None) if False else 1.0
    startedB = False
    while pA is not None or pB is not None:
        if pB is None or (pA is not None and pA <= pB):
            pA = next(gA, None)
        else:
            pB = next(gB, None)

    P.finish("sp")
    P.emit()
    es.close()
    return nc, P


def _pack_params(inp, L):
    def chunks(v, n):
        return np.ascontiguousarray(v.reshape(n, 128).T)
    par = np.zeros((128, L * NP + 8), np.float32)
    for l in range(L):
        b = l * NP
        par[:, b + P_G1:b + P_G1 + 8] = chunks(inp["norm1_g"][l], 8)
        par[:, b + P_G2:b + P_G2 + 8] = chunks(inp["norm2_g"][l], 8)
        caw = inp["conv_a_w"][l]
        par[:, b + P_CAW:b + P_CAW + 124] = caw.T.reshape(4, 128, CAW).transpose(1, 0, 2).reshape(128, 124)
        par[:, b + P_CAB:b + P_CAB + 4] = chunks(inp["conv_a_b"][l], 4)
        par[:, b + P_LNG:b + P_LNG + 4] = chunks(inp["ln_a_g"][l], 4)
        par[:, b + P_LNB:b + P_LNB + 4] = chunks(inp["ln_a_b"][l], 4)
        cbw = inp["conv_b_w"][l]
        par[:, b + P_CBW:b + P_CBW + 16] = cbw.T.reshape(4, 128, CBW).transpose(1, 0, 2).reshape(128, 16)
        par[:, b + P_CBB:b + P_CBB + 4] = chunks(inp["conv_b_b"][l], 4)
        par[:, b + P_GRB:b + P_GRB + 4] = chunks(inp["gate_r_b"][l], 4)
        par[:, b + P_GIB:b + P_GIB + 4] = chunks(inp["gate_i_b"][l], 4)
        par[:, b + P_LAM:b + P_LAM + 4] = chunks(inp["rglru_lambda"][l], 4)
    par[:, L * NP:L * NP + 8] = chunks(inp["final_norm_g"], 8)
    return par


def _chan_in(a):
    if a.ndim == 2:
        L = a.shape[0]
        return np.ascontiguousarray(a.reshape(L, 4, 128).transpose(2, 0, 1))
    L, n, _ = a.shape
    return np.ascontiguousarray(a.reshape(L, n, 4, 128).transpose(3, 0, 2, 1))


def _chan_out(a):
    if a.ndim == 3:
        L = a.shape[1]
        return np.ascontiguousarray(a.transpose(1, 2, 0).reshape(L, 512))
    L, n = a.shape[1], a.shape[3]
    return np.ascontiguousarray(a.transpose(1, 3, 2, 0).reshape(L, n, 512))


_CACHE = {}


def run(inp, cfg, n_cores=N_CORES):
    L, NT, T, TS = cfg["L"], cfg["NT"], cfg["T"], cfg["TS"]
    SEQ = NT * T
    key = tuple(sorted(cfg.items()))
    if key not in _CACHE:
        _CACHE[key] = build(cfg)[0]
    nc = _CACHE[key]
    f32 = lambda a: np.ascontiguousarray(np.asarray(a, dtype=np.float32))
    inp = {k: f32(v) for k, v in inp.items()}
    par = _pack_params(inp, L)
    in_maps = []
    for c in range(n_cores):
        xp = inp["x_prompt"][2 * c:2 * c + 2]
        xT = np.ascontiguousarray(xp.transpose(2, 0, 1).reshape(D, 2 * SEQ))
        xsT = np.ascontiguousarray(inp["x_sample"][c].T)
        in_maps.append({
            "xT": xT, "xsT": xsT,
            "sca": _chan_in(inp["state_conv_a"][:, c]),
            "scb": _chan_in(inp["state_conv_b"][:, c]),
            "srg": _chan_in(inp["state_rglru"][:, c]),
            "par": par,
            "w_in": inp["w_in"], "w_out": inp["w_out"], "w_fin": inp["w_ffn_in"], "w_fout": inp["w_ffn_out"],
            "gate_r": inp["gate_r_w"], "gate_i": inp["gate_i_w"],
            "ident": np.eye(128, dtype=np.float32),
        })
    res = run_bass_kernel_spmd(nc, in_maps, core_ids=list(range(n_cores)))
    B = 2 * n_cores
    y_prompt = np.zeros((B, SEQ, D), np.float32)
    y_sample = np.zeros((n_cores, TS, D), np.float32)
    pa = np.zeros((L, B, CAW - 1, DC), np.float32)
    pb = np.zeros((L, B, CBW - 1, DC), np.float32)
    ph = np.zeros((L, B, DC), np.float32)
    sa = np.zeros((L, n_cores, CAW - 1, DC), np.float32)
    sbo = np.zeros((L, n_cores, CBW - 1, DC), np.float32)
    sh = np.zeros((L, n_cores, DC), np.float32)
    for c in range(n_cores):
        r = res.results[c]
        y_prompt[2 * c:2 * c + 2] = np.asarray(r["yT"]).reshape(D, 2, SEQ).transpose(1, 2, 0)
        y_sample[c] = np.asarray(r["ysT"]).T
        for s in range(2):
            pa[:, 2 * c + s] = _chan_out(np.asarray(r["pa"])[:, :, s])
            pb[:, 2 * c + s] = _chan_out(np.asarray(r["pb"])[:, :, s])
            ph[:, 2 * c + s] = _chan_out(np.asarray(r["ph"])[:, :, s])
        sa[:, c] = _chan_out(np.asarray(r["sa"]))
        sbo[:, c] = _chan_out(np.asarray(r["sb"]))
        sh[:, c] = _chan_out(np.asarray(r["sh"]))
    return (y_prompt, y_sample, pa, pb, ph, sa, sbo, sh)


FULL_CFG = {"L": 4, "NT": 8, "T": 512, "TS": 64}


def kernel(**inputs):
    return run(inputs, FULL_CFG)
```

```python
import contextlib
import numpy as np
import concourse.bass as bass
import concourse.mybir as mybir
from concourse.bass_utils import run_bass_kernel_spmd

F32 = mybir.dt.float32
BF16 = mybir.dt.bfloat16
AF = mybir.ActivationFunctionType
ALU = mybir.AluOpType

D = 1024
KC = 8
DC = 512
JC = 4
DFF = 2816
FC = 22
FH = 11
CAW = 31
CBW = 4
RMS_EPS = 1e-6
LN_EPS = 1e-5
N_CORES = 8

P_G1, P_G2, P_CAW, P_CAB, P_LNG, P_LNB, P_CBW, P_CBB, P_GRB, P_GIB, P_LAM = (
    0, 8, 16, 140, 144, 148, 152, 168, 172, 176, 180)
NP = 184
Q_HRB, Q_HIB, Q_CL, Q_HCL = 0, 4, 8, 12
NQ = 16
W_IN0, W_OUT0, W_FIN0, W_FOUT0, W_DG0, W_PER_L = 0, 8, 12, 34, 50, 52
NPE = 5
WT_COLS = 2048


class Prog:
    ENGS = ("pe", "act", "dve", "pool", "sp")

    def __init__(self, nc, es):
        self.nc = nc
        self.es = es
        self.streams = {e: [] for e in self.ENGS}
        self.sems = {e: es.enter_context(nc.semaphore("s_" + e)) for e in self.ENGS}
        self.cnt = {e: 0 for e in self.ENGS}
        self.known = {e: {} for e in self.ENGS}
        self.res = {}
        self.chan = {}
        self.semobj = {}
        for e in self.ENGS:
            self.semobj[id(self.sems[e])] = self.sems[e]
        self.own = {e: id(self.sems[e]) for e in self.ENGS}
        self.nwaits = 0
        self.dry = False

    def _waits(self, eng, toks):
        need = {}
        for (sid, val) in toks:
            if need.get(sid, 0) < val:
                need[sid] = val
        for sid, val in need.items():
            if sid == self.own[eng]:
                if eng in ("pe", "sp"):
                    continue
            if self.known[eng].get(sid, 0) >= val:
                continue
            self.known[eng][sid] = val
            sem = self.semobj[sid]
            self.streams[eng].append(("w", sem, val))
            self.nwaits += 1

    def _deps(self, reads, writes):
        toks = []
        for k in reads:
            r = self.res.get(k)
            if r is not None and r[0] is not None:
                toks.append(r[0])
        for k in writes:
            r = self.res.get(k)
            if r is not None:
                if r[0] is not None:
                    toks.append(r[0])
                toks.extend(r[1].values())
        return toks

    def _record(self, reads, writes, tok):
        for k in reads:
            r = self.res.get(k)
            if r is None:
                r = [None, {}]
                self.res[k] = r
            old = r[1].get(tok[0])
            if old is None or old[1] < tok[1]:
                r[1][tok[0]] = tok
        for k in writes:
            self.res[k] = [tok, {}]

    def op(self, eng, fn, reads=(), writes=(), inc=True):
        if self.dry:
            return
        self._waits(eng, self._deps(reads, writes))
        if inc:
            self.cnt[eng] += 1
            tok = (self.own[eng], self.cnt[eng])
            self.streams[eng].append(("i", fn, self.sems[eng]))
        else:
            tok = (self.own[eng], self.cnt[eng] + 1)
            self.streams[eng].append(("n", fn))
        self._record(reads, writes, tok)

    def dma(self, q, chan, out, in_, reads=(), writes=()):
        if self.dry:
            return
        c = self.chan.get(chan)
        if c is None:
            sem = self.es.enter_context(self.nc.semaphore("d_" + str(len(self.chan))))
            self.semobj[id(sem)] = sem
            c = [sem, 0]
            self.chan[chan] = c
        toks = self._deps(reads, writes)
        if c[1] > 0:
            toks.append((id(c[0]), c[1]))
        self._waits(q, toks)
        c[1] += 16
        tok = (id(c[0]), c[1])
        self.streams[q].append(("d", out, in_, c[0]))
        self._record(reads, writes, tok)

    def finish(self, q="sp"):
        toks = [(id(c[0]), c[1]) for c in self.chan.values() if c[1] > 0]
        self._waits(q, toks)

    def emit(self):
        nc = self.nc
        block = self.es.enter_context(nc.Block())
        engmap = {"pe": block.tensor, "act": block.scalar, "dve": block.vector,
                  "pool": block.gpsimd, "sp": block.sync}
        for ename in self.ENGS:
            items = self.streams[ename]

            def body(e, items=items):
                for it in items:
                    k = it[0]
                    if k == "w":
                        e.wait_ge(it[1], it[2])
                    elif k == "i":
                        it[1](e).then_inc(it[2], 1)
                    elif k == "n":
                        it[1](e)
                    else:
                        e.dma_start(out=it[1], in_=it[2]).then_inc(it[3], 16)
            engmap[ename](body)


class Rot:
    def __init__(self, items):
        self.items = items
        self.i = 0

    def __call__(self):
        it = self.items[self.i % len(self.items)]
        self.i += 1
        return it


class Stream:
    pass


def build(cfg):
    L = cfg["L"]
    NT = cfg["NT"]
    T = cfg["T"]
    TS = cfg["TS"]
    SEQ = NT * T
    NW = L * W_PER_L
    NRING = cfg.get("NRING", 7)

    nc = bass.Bass("TRN2", target_bir_lowering=False)
    es = contextlib.ExitStack()

    def din(name, shape, dt=F32):
        return nc.dram_tensor(name, list(shape), dt, kind="ExternalInput").ap()

    def dout(name, shape):
        return nc.dram_tensor(name, list(shape), F32, kind="ExternalOutput").ap()

    xT = din("xT", [D, 2 * SEQ])
    xsT = din("xsT", [D, TS])
    sca_in = din("sca", [128, L, JC, CAW - 1])
    scb_in = din("scb", [128, L, JC, CBW - 1])
    srg_in = din("srg", [128, L, JC])
    par_in = din("par", [128, L * NP + 8])
    w_in = din("w_in", [L, D, 4 * DC])
    w_out = din("w_out", [L, D, D])
    w_fin = din("w_fin", [L, D, 2 * DFF])
    w_fout = din("w_fout", [L, DFF, D])
    gr_in = din("gate_r", [L, 8, 64, 64])
    gi_in = din("gate_i", [L, 8, 64, 64])
    ident_in = din("ident", [128, 128])

    yT = dout("yT", [D, 2 * SEQ])
    ysT = dout("ysT", [D, TS])
    pa_o = dout("pa", [128, L, 2, JC, CAW - 1])
    pb_o = dout("pb", [128, L, 2, JC, CBW - 1])
    ph_o = dout("ph", [128, L, 2, JC])
    sa_o = dout("sa", [128, L, JC, CAW - 1])
    sb_o = dout("sb", [128, L, JC, CBW - 1])
    sh_o = dout("sh", [128, L, JC])

    wsc = nc.dram_tensor("wsc", [NW, 128, WT_COLS], BF16).ap()

    def sb(name, shape, dt=F32):
        return es.enter_context(nc.sbuf_tensor("sb_" + name, list(shape), dt))

    P = Prog(nc, es)

    NXB = 3
    xres = [sb("xres%d" % s, [128, KC, T]) for s in range(NXB)]
    actb = [sb("actb%d" % s, [128, KC, T], BF16) for s in range(2)]
    rstd = [sb("rstd%d" % s, [128, T]) for s in range(2)]
    ust = [sb("ust%d" % s, [128, L, JC, CAW - 1]) for s in range(2)]
    xbst = [sb("xbst%d" % s, [128, L, JC, CBW - 1]) for s in range(2)]
    hst = [sb("hst%d" % s, [128, L, JC]) for s in range(2)]
    u_t = sb("u", [128, JC, CAW - 1 + T])
    ca_t = sb("ca", [128, JC, T])
    cab_t = sb("cab", [128, 2 * JC, T], BF16)
    xb_t = sb("xb", [128, JC, CBW - 1 + T])
    gb_t = sb("gb", [128, JC, T], BF16)
    xcb_t = sb("xcb", [128, 2, T])
    xcbb_t = sb("xcbb", [128, 2, T], BF16)
    P_t = sb("Pt", [128, 2, T])
    Q_t = sb("Qt", [128, 2, T])
    R_t = sb("Rt", [128, 2, T])
    ubf_t = sb("ubf", [128, JC, CAW - 1 + T], BF16)
    ident_t = sb("ident", [128, 128])
    lnt_t = sb("lnt", [128, 2, T])
    sq8 = {"M": sb("sqM", [128, KC, T], BF16), "F": sb("sqF", [128, KC, T], BF16)}
    hh_t = sb("hh", [128, FH, T], BF16)
    sg_t = sb("sg", [128, 2, T])
    ring_t = sb("ring", [128, NRING, WT_COLS], BF16)
    par_t = sb("par", [128, L * NP + 8])
    dpar_t = sb("dpar", [128, L, NQ])
    bdg_t = sb("bdg", [128, L * 8, 128], BF16)
    ones_t = sb("ones", [128, 128], BF16)
    psum = [es.enter_context(nc.psum_tensor("ps%d" % i, [128, 512], F32)) for i in range(8)]

    psM = Rot([(psum[i], ("ps", i)) for i in range(0, 3)])
    psF = Rot([(psum[i], ("ps", i)) for i in range(3, 8)])
    thR = Rot([(lnt_t[:, i, :], ("lnt", i)) for i in range(2)])
    sgR = Rot([(sg_t[:, i, :], ("sg", i)) for i in range(2)])
    ring_state = {"i": 0}

    def pcol(l, c0, n=1):
        return par_t[:, l * NP + c0: l * NP + c0 + n]

    def qcol(l, c0, n=1):
        return dpar_t[:, l, c0:c0 + n]

    def A(eng_out, in_, func, reads, writes, bias=0.0, scale=1.0):
        P.op("act", lambda e: e.activation(out=eng_out, in_=in_, func=func, bias=bias, scale=scale),
             reads, writes)

    def MM(out, lhsT, rhs, start, stop, reads, writes, inc=None):
        P.op("pe", lambda e: e.matmul(out, lhsT, rhs, start=start, stop=stop), reads, writes,
             inc=(stop if inc is None else inc))

    def STT(out, in0, scalar, in1, op0, op1, reads, writes):
        P.op("dve", lambda e: e.scalar_tensor_tensor(out=out, in0=in0, scalar=scalar, in1=in1, op0=op0, op1=op1),
             reads, writes)

    def TT(out, in0, in1, op, reads, writes):
        P.op("dve", lambda e: e.tensor_tensor(out=out, in0=in0, in1=in1, op=op), reads, writes)

    def TS2(out, in0, s1, s2, op0, op1, reads, writes, eng="dve"):
        P.op(eng, lambda e: e.tensor_scalar(out=out, in0=in0, scalar1=s1, scalar2=s2, op0=op0, op1=op1),
             reads, writes)

    P.dma("sp", "par", par_t[:], par_in, writes=[("par",)])
    P.op("dve", lambda e: e.memset(ones_t[:], 1.0), writes=[("ones",)])
    P.dma("sp", "ident", ident_t[:], ident_in, writes=[("ident",)])
    for l in range(L):
        A(qcol(l, Q_CL, 4), pcol(l, P_LAM, 4), AF.Exp, [("par",)], [("dpar", l)], scale=-1.0)
        A(qcol(l, Q_CL, 4), qcol(l, Q_CL, 4), AF.Ln, [("dpar", l)], [("dpar", l)], bias=1.0)
        P.op("act", lambda e, l=l: e.mul(qcol(l, Q_HCL, 4), qcol(l, Q_CL, 4), -4.0), [("dpar", l)], [("dpar", l)])
        P.op("act", lambda e, l=l: e.mul(qcol(l, Q_CL, 4), qcol(l, Q_CL, 4), -8.0), [("dpar", l)], [("dpar", l)])
        P.op("act", lambda e, l=l: e.mul(qcol(l, Q_HRB, 4), pcol(l, P_GRB, 4), 0.5), [("par",)], [("dpar", l)])
        P.op("act", lambda e, l=l: e.mul(qcol(l, Q_HIB, 4), pcol(l, P_GIB, 4), 0.5), [("par",)], [("dpar", l)])
    assert L * 8 * 128 <= KC * T
    xk1 = [("x", 1, k) for k in range(KC)]
    stage_flat = xres[1][:].rearrange("p k t -> p (k t)")
    bd_stage = stage_flat[:, 0:L * 8 * 128].rearrange("p (m q) -> p m q", q=128)
    P.op("dve", lambda e: e.memset(bd_stage, 0.0), writes=xk1)
    for l in range(L):
        for g, gin in enumerate((gr_in, gi_in)):
            v = gin[l].rearrange("(c two) i j -> two i c j", two=2)
            m0 = (l * 2 + g) * 4
            P.dma("sp", ("bd", (l * 2 + g) % 4, 0), bd_stage[0:64, m0:m0 + 4, 0:64], v[0], writes=xk1)
            P.dma("sp", ("bd", (l * 2 + g) % 4, 1), bd_stage[64:128, m0:m0 + 4, 64:128], v[1], writes=xk1)
    P.op("dve", lambda e: e.tensor_copy(out=bdg_t[:], in_=bd_stage), reads=xk1, writes=[("bdg",)])

    def wsrc(l, t):
        if t < W_OUT0:
            i = t - W_IN0
            sec, j = (0, i) if i < 4 else (2, i - 4)
            v = w_in[l].rearrange("(k p) (s j q) -> p k s j q", p=128, s=4, j=4, q=128)
            return v[:, :, sec:sec + 2, j, :], 2048, ("inA" if i < 4 else "inB")
        if t < W_FIN0:
            i = t - W_OUT0
            v = w_out[l].rearrange("(k p) c -> p k c", p=128)
            return v[:, :, 256 * i:256 * i + 256], 2048, "out"
        if t < W_FOUT0:
            f = t - W_FIN0
            v = w_fin[l].rearrange("(k p) (s f q) -> p k s f q", p=128, s=2, f=FC, q=128)
            return v[:, :, :, f, :], 2048, "fin"
        i = t - W_FOUT0
        hf, d = i // 8, i % 8
        v = w_fout[l].rearrange("(h kk p) (d q) -> p h kk d q", h=2, kk=FH, p=128, d=8, q=128)
        return v[:, hf, :, d, :], FH * 128, "fout"

    cv_flat = xres[2][:].rearrange("p k t -> p (k t)")
    assert KC * T >= 4096
    cv_stage = []
    for h in range(2):
        keys = [("x", 2, k) for k in range(KC) if (k * T < (h + 1) * 2048 and (k + 1) * T > h * 2048)]
        cv_stage.append((cv_flat[:, h * 2048:(h + 1) * 2048], keys))
    cv_stage.append((ring_t[:, NRING - 2:NRING, :].rearrange("p a b -> p (a b)").bitcast(F32),
                     [("w", NRING - 2), ("w", NRING - 1)]))
    NCV = len(cv_stage)
    cv_state = {"n": 0, "active": True}

    cv_seq = []
    cv_slot = {}
    cv_state.update({"loaded": 0, "cast": 0})
    LOADAHEAD, CASTAHEAD = 2, 1

    def tile_view(slot, t):
        if t >= W_DG0:
            return ring_t[:, slot, 0:2 * NPE * 128].rearrange("p (a b) -> p a b", b=128)
        if t >= W_FOUT0:
            return ring_t[:, slot, 0:FH * 128].rearrange("p (a b) -> p a b", b=128)
        return ring_t[:, slot, :].rearrange("p (a b) -> p a b", b=256)

    def next_slot():
        slot = ring_state["i"] % (NRING - 4 if cv_state["active"] else NRING)
        ring_state["i"] += 1
        return slot

    def cv_issue_load(i):
        l, t = cv_seq[i]
        if t >= W_DG0:
            return
        src, ncols, kind = wsrc(l, t)
        stg, keys = cv_stage[i % NCV]
        if kind == "fout":
            P.dma("sp", ("pl", i % NCV, 0), stg[:, 0:ncols].rearrange("p (a b) -> p a b", b=128), src, writes=keys)
        elif kind == "out":
            P.dma("sp", ("pl", i % NCV, 0), stg.rearrange("p (a b) -> p a b", b=256), src, writes=keys)
        else:
            dst = stg.rearrange("p (a s b) -> p a s b", s=2, b=128)
            P.dma("sp", ("pl", i % NCV, 0), dst[:, :, 0, :], src[:, :, 0, :], writes=keys)
            P.dma("sp", ("pl", i % NCV, 1), dst[:, :, 1, :], src[:, :, 1, :], writes=keys)

    def cv_issue_cast(i):
        l, t = cv_seq[i]
        slot = NRING - 4 + (i % 2)
        cv_slot[i] = slot
        wkey = ("w", slot)
        if t >= W_DG0:
            ci = t - W_DG0
            for jj in range(2):
                j = 2 * ci + jj
                for k in range(NPE):
                    m = jj * NPE + k
                    TS2(ring_t[:, slot, m * 128:(m + 1) * 128], ident_t[:], pcol(l, P_CAW + j * CAW + k), None,
                        ALU.mult, ALU.bypass, [("ident",), ("par",)], [wkey])
            ncols = 2 * NPE * 128
        else:
            src, ncols, kind = wsrc(l, t)
            stg, keys = cv_stage[i % NCV]
            if kind == "inA":
                sv = stg.rearrange("p (a s b) -> p a s b", s=2, b=128)
                dv = ring_t[:, slot, :].rearrange("p (a s b) -> p a s b", s=2, b=128)
                TS2(dv[:, :, 0, :], sv[:, :, 0, :], 0.5, None, ALU.mult, ALU.bypass, keys, [wkey])
                P.op("dve", lambda e: e.tensor_copy(out=dv[:, :, 1, :], in_=sv[:, :, 1, :]), keys, [wkey])
            else:
                o = ring_t[:, slot, 0:ncols]
                i_ = stg[:, 0:ncols]
                P.op("dve", lambda e: e.tensor_copy(out=o, in_=i_), keys, [wkey])
        P.dma("pool", ("pst", slot), wsc[l * W_PER_L + t][:, 0:ncols], ring_t[:, slot, 0:ncols],
              reads=[wkey], writes=[("wsc", l, t)])

    def wfetch(st, l, t):
        if P.dry:
            if st.convert:
                cv_seq.append((l, t))
            return tile_view(0, t), ("w", 0)
        if st.convert:
            n = cv_state["n"]
            cv_state["n"] += 1
            assert cv_seq[n] == (l, t), (n, cv_seq[n], l, t)
            while cv_state["loaded"] < min(n + LOADAHEAD + 1, len(cv_seq)):
                cv_issue_load(cv_state["loaded"])
                cv_state["loaded"] += 1
            while cv_state["cast"] < min(n + CASTAHEAD + 1, len(cv_seq)):
                cv_issue_cast(cv_state["cast"])
                cv_state["cast"] += 1
            slot = cv_slot[n]
            return tile_view(slot, t), ("w", slot)
        slot = next_slot()
        wkey = ("w", slot)
        ncols = FH * 128 if W_FOUT0 <= t < W_DG0 else (2 * NPE * 128 if t >= W_DG0 else 2048)
        P.dma("sp", ("ring", slot), ring_t[:, slot, 0:ncols], wsc[l * W_PER_L + t][:, 0:ncols],
              reads=[("wsc", l, t)], writes=[wkey])
        return tile_view(slot, t), wkey

    def sq_step(st, which, k):
        Tt, xb = st.T, st.xb
        buf = sq8[which]
        A(buf[:, k, :Tt], xres[xb][:, k, :Tt], AF.Square, [("x", xb, k)], [("sq", which, k)])

    def rms_from_sq(st, which, psR):
        s, Tt = st.s, st.T
        bank, bkey = psR()
        buf = sq8[which]
        for k in range(KC):
            MM(bank[:, :Tt], ones_t[:], buf[:, k, :Tt], k == 0, k == KC - 1, [("sq", which, k), ("ones",)], [bkey])
        A(rstd[s][:, :Tt], bank[:, :Tt], AF.Ln, [bkey], [("rstd", s)], bias=RMS_EPS, scale=1.0 / D)
        A(rstd[s][:, :Tt], rstd[s][:, :Tt], AF.Exp, [("rstd", s)], [("rstd", s)], scale=-0.5)

    def norm_to_act(st, l, gcol):
        s, Tt, xb = st.s, st.T, st.xb
        for k in range(KC):
            STT(actb[s][:, k, :Tt], xres[xb][:, k, :Tt], pcol(l, gcol + k), rstd[s][:, :Tt], ALU.mult, ALU.mult,
                [("x", xb, k), ("rstd", s), ("par",)], [("act", s, k)])

    def x_load(st):
        Tt, xb = st.T, st.xb
        P.dma("pool", ("xl", xb), xres[xb][:, :, :Tt],
              st.xsrc.rearrange("(k p) t -> p k t", p=128), writes=[("x", xb, k) for k in range(KC)])
        st.loaded = True

    def tile_prologue(st):
        s, Tt = st.s, st.T
        if not st.loaded:
            x_load(st)
        if st.first:
            if st.sample:
                P.dma("pool", ("sti", 0), ust[s][:], sca_in, writes=[("ust", s, l) for l in range(L)])
                P.dma("pool", ("sti", 1), xbst[s][:], scb_in, writes=[("xbst", s, l) for l in range(L)])
                P.dma("pool", ("sti", 2), hst[s][:], srg_in, writes=[("hst", s, l) for l in range(L)])
            else:
                P.op("act", lambda e: e.memzero(ust[s][:]), writes=[("ust", s, l) for l in range(L)])
                P.op("act", lambda e: e.memzero(xbst[s][:]), writes=[("xbst", s, l) for l in range(L)])

    def task_M(st, l):
        s, Tt, xb = st.s, st.T, st.xb
        tot = 186.0
        prog = 0.0
        if l == 0:
            tile_prologue(st)
            for k in range(KC):
                sq_step(st, "F", k)
        P.op("act", lambda e: e.copy(u_t[:, :, 0:CAW - 1], ust[s][:, l]), [("ust", s, l)], [("uh",)])
        P.op("act", lambda e: e.copy(ubf_t[:, :, 0:CAW - 1], ust[s][:, l]), [("ust", s, l)], [("ubh",)])
        P.op("act", lambda e: e.copy(xb_t[:, :, 0:CBW - 1], xbst[s][:, l]), [("xbst", s, l)], [("xbh",)])
        rms_from_sq(st, "F", psM)
        prog += 3
        yield prog / tot
        norm_to_act(st, l, P_G1)
        prog += 8
        yield prog / tot

        def pairA(j):
            wv, wk = wfetch(st, l, W_IN0 + j)
            bv, bvk = psM()
            bg, bgk = psM()
            for k in range(KC):
                MM(bv[:, :Tt], wv[:, k, 0:128], actb[s][:, k, :Tt], k == 0, k == KC - 1, [wk, ("act", s, k)], [bvk])
            for k in range(KC):
                MM(bg[:, :Tt], wv[:, k, 128:256], actb[s][:, k, :Tt], k == 0, k == KC - 1, [wk, ("act", s, k)], [bgk])
            th, thk = thR()
            A(th[:, :Tt], bg[:, :Tt], AF.Tanh, [bgk], [thk], scale=0.5)
            STT(u_t[:, j, CAW - 1:CAW - 1 + Tt], th[:, :Tt], 1.0, bv[:, :Tt], ALU.add, ALU.mult,
                [thk, bvk], [("u", j)])

        def pairB(j):
            wv, wk = wfetch(st, l, W_IN0 + 4 + j)
            bx, bxk = psM()
            bt, btk = psM()
            for k in range(KC):
                MM(bx[:, :Tt], wv[:, k, 0:128], actb[s][:, k, :Tt], k == 0, k == KC - 1, [wk, ("act", s, k)], [bxk])
            for k in range(KC):
                MM(bt[:, :Tt], wv[:, k, 128:256], actb[s][:, k, :Tt], k == 0, k == KC - 1, [wk, ("act", s, k)], [btk])
            P.op("act", lambda e: e.copy(xb_t[:, j, CBW - 1:CBW - 1 + Tt], bx[:, :Tt]), [bxk], [("xb", j)])
            A(gb_t[:, j, :Tt], bt[:, :Tt], AF.Gelu_apprx_tanh, [btk], [("gb", j)])

        def convA(js):
            for j in js:
                P.op("act", lambda e, j=j: e.copy(ubf_t[:, j, CAW - 1:CAW - 1 + Tt], u_t[:, j, CAW - 1:CAW - 1 + Tt]),
                     [("u", j)], [("ubf", j)])
            yield 0
            dv, dk = wfetch(st, l, W_DG0 + js[0] // 2)
            for jj, j in enumerate(js):
                bank, bkey = psM()
                for k in range(NPE):
                    MM(bank[:, :Tt], dv[:, jj * NPE + k, :], ubf_t[:, j, k:k + Tt], k == 0, k == NPE - 1,
                       [dk, ("ubf", j), ("ubh",)], [bkey])
                TS2(ca_t[:, j, :Tt], bank[:, :Tt], pcol(l, P_CAB + j), None, ALU.add, ALU.bypass,
                    [bkey, ("par",)], [("ca", j)])
            yield 2
            for k in range(NPE, CAW):
                for j in js:
                    src = u_t[:, j, k:k + Tt]
                    w = pcol(l, P_CAW + j * CAW + k)
                    STT(ca_t[:, j, :Tt], src, w, ca_t[:, j, :Tt], ALU.mult, ALU.add,
                        [("u", j), ("uh",), ("ca", j)], [("ca", j)])
                yield 2

        def chainB(gq):
            js = (2 * gq, 2 * gq + 1)
            for kk in range(CBW):
                for j in js:
                    q = j % 2
                    src = xb_t[:, j, kk:kk + Tt]
                    w = pcol(l, P_CBW + j * CBW + kk)
                    if kk == 0:
                        TS2(xcb_t[:, q, :Tt], src, w, pcol(l, P_CBB + j), ALU.mult, ALU.add,
                            [("xb", j), ("xbh",), ("par",)], [("xcb", q)])
                    else:
                        STT(xcb_t[:, q, :Tt], src, w, xcb_t[:, q, :Tt], ALU.mult, ALU.add,
                            [("xb", j), ("xbh",), ("xcb", q)], [("xcb", q)])
            for j in js:
                q = j % 2
                P.op("dve", lambda e, q=q: e.tensor_copy(out=xcbb_t[:, q, :Tt], in_=xcb_t[:, q, :Tt]),
                     [("xcb", q)], [("xcbb", q)])
            yield 9
            yield 0
            for j in js:
                q = j % 2
                pr, prk = psM()
                pi, pik = psM()
                MM(pr[:, :Tt], bdg_t[:, (l * 2 + 0) * 4 + j, :], xcbb_t[:, q, :Tt], True, True,
                   [("bdg",), ("xcbb", q)], [prk])
                MM(pi[:, :Tt], bdg_t[:, (l * 2 + 1) * 4 + j, :], xcbb_t[:, q, :Tt], True, True,
                   [("bdg",), ("xcbb", q)], [pik])
                A(P_t[:, q, :Tt], pr[:, :Tt], AF.Tanh, [prk, ("dpar", l)], [("P", q)],
                  bias=qcol(l, Q_HRB + j), scale=0.5)
                A(Q_t[:, q, :Tt], pi[:, :Tt], AF.Tanh, [pik, ("dpar", l)], [("Q", q)],
                  bias=qcol(l, Q_HIB + j), scale=0.5)
            yield 0
            for j in js:
                q = j % 2
                A(R_t[:, q, :Tt], P_t[:, q, :Tt], AF.Exp, [("P", q)], [("R", q)],
                  bias=qcol(l, Q_HCL + j), scale=qcol(l, Q_HCL + j))
                A(P_t[:, q, :Tt], P_t[:, q, :Tt], AF.Exp, [("P", q)], [("P", q)],
                  bias=qcol(l, Q_CL + j), scale=qcol(l, Q_CL + j))
            yield 0
            for j in js:
                q = j % 2
                A(P_t[:, q, :Tt], P_t[:, q, :Tt], AF.Relu, [("P", q)], [("P", q)], bias=1.0, scale=-1.0)
            for j in js:
                q = j % 2
                A(P_t[:, q, :Tt], P_t[:, q, :Tt], AF.Ln, [("P", q)], [("P", q)], bias=1e-18)
                A(P_t[:, q, :Tt], P_t[:, q, :Tt], AF.Exp, [("P", q)], [("P", q)], bias=float(np.log(0.5)), scale=0.5)
            yield 0
            yield 0
            for j in js:
                q = j % 2
                STT(Q_t[:, q, :Tt], Q_t[:, q, :Tt], 1.0, xcb_t[:, q, :Tt], ALU.add, ALU.mult,
                    [("Q", q), ("xcb", q)], [("Q", q)])
            if st.reset:
                for j in js:
                    q = j % 2
                    P.op("dve", lambda e, q=q: e.memset(P_t[:, q, 0:1], 0.5), [("P", q)], [("P", q)])
            for j in js:
                q = j % 2
                TT(Q_t[:, q, :Tt], Q_t[:, q, :Tt], P_t[:, q, :Tt], ALU.mult, [("Q", q), ("P", q)], [("Q", q)])
            for j in js:
                q = j % 2
                init = 0.0 if st.reset else hst[s][:, l, j:j + 1]
                P.op("dve", lambda e, q=q, init=init: e.tensor_tensor_scan(
                    out=P_t[:, q, :Tt], data0=R_t[:, q, :Tt], data1=Q_t[:, q, :Tt], initial=init,
                    op0=ALU.mult, op1=ALU.add), [("R", q), ("Q", q), ("hst", s, l)], [("P", q)])
            for j in js:
                q = j % 2
                P.op("act", lambda e, q=q, j=j: e.copy(hst[s][:, l, j:j + 1], P_t[:, q, Tt - 1:Tt]),
                     [("P", q)], [("hst", s, l)])
                TT(actb[s][:, JC + j, :Tt], P_t[:, q, :Tt], gb_t[:, j, :Tt], ALU.mult,
                   [("P", q), ("gb", j)], [("act", s, JC + j)])
            yield 10

        def side0():
            pairB(0); yield 0.5
            pairB(1); yield 0.5
            gb0 = chainB(0)
            yield next(gb0)
            pairA(2); yield 1
            yield next(gb0)
            yield next(gb0)
            pairA(3); yield 1
            yield next(gb0)
            pairB(2); yield 0.5
            yield next(gb0)
            pairB(3); yield 0.5
            yield next(gb0)
            for w_ in gb0:
                yield w_

        def interleave(main, side, every):
            nonlocal prog
            n = 0
            sdone = False
            for wgt in main:
                prog += wgt
                n += 1
                if not sdone and n % every == 0:
                    try:
                        prog += next(side)
                    except StopIteration:
                        sdone = True
                yield prog / tot
            while not sdone:
                try:
                    prog += next(side)
                    yield prog / tot
                except StopIteration:
                    sdone = True

        pairA(0)
        prog += 1
        yield prog / tot
        pairA(1)
        prog += 1
        yield prog / tot
        for fr in interleave(convA((0, 1)), side0(), 2):
            yield fr
        def cab_ops(js):
            for j in js:
                P.op("act", lambda e, j=j: e.copy(cab_t[:, j, :Tt], ca_t[:, j, :Tt]), [("ca", j)], [("cab", j)])
                A(cab_t[:, JC + j, :Tt], ca_t[:, j, :Tt], AF.Square, [("ca", j)], [("cab", JC + j)])

        def side1():
            yield 0
            cab_ops((0, 1))
            for w_ in chainB(1):
                yield w_

        for fr in interleave(convA((2, 3)), side1(), 3):
            yield fr
        P.op("act", lambda e: e.copy(xbst[s][:, l], xb_t[:, :, Tt:Tt + CBW - 1]),
             [("xb", j) for j in range(JC)] + [("xbh",)], [("xbst", s, l)])
        if st.last:
            P.dma("pool", ("sto", 1), st.pb_dst(l), xbst[s][:, l], reads=[("xbst", s, l)])
            P.dma("pool", ("sto", 2), st.ph_dst(l), hst[s][:, l], reads=[("hst", s, l)])
        P.op("act", lambda e: e.copy(ust[s][:, l], u_t[:, :, Tt:Tt + CAW - 1]),
             [("u", j) for j in range(JC)] + [("uh",)], [("ust", s, l)])
        if st.last:
            P.dma("pool", ("sto", 0), st.pa_dst(l), ust[s][:, l], reads=[("ust", s, l)])
        cab_ops((2, 3))
        prog += 4
        yield prog / tot
        s1, s1k = psM()
        s2, s2k = psM()
        for j in range(JC):
            MM(s1[:, :Tt], ones_t[:], cab_t[:, j, :Tt], j == 0, j == JC - 1, [("cab", j), ("ones",)], [s1k])
        for j in range(JC):
            MM(s2[:, :Tt], ones_t[:], cab_t[:, JC + j, :Tt], j == 0, j == JC - 1, [("cab", JC + j), ("ones",)], [s2k])
        l0 = lnt_t[:, 0, :Tt]
        l1 = lnt_t[:, 1, :Tt]
        A(l0, s1[:, :Tt], AF.Square, [s1k], [("lnt", 0)], scale=1.0 / DC)
        STT(l1, s2[:, :Tt], 1.0 / DC, l0, ALU.mult, ALU.subtract, [s2k, ("lnt", 0)], [("lnt", 1)])
        A(l1, l1, AF.Relu, [("lnt", 1)], [("lnt", 1)])
        A(l1, l1, AF.Ln, [("lnt", 1)], [("lnt", 1)], bias=LN_EPS)
        A(l1, l1, AF.Exp, [("lnt", 1)], [("lnt", 1)], scale=-0.5)
        STT(l0, s1[:, :Tt], -1.0 / DC, l1, ALU.mult, ALU.mult, [s1k, ("lnt", 1), ("lnt", 0)], [("lnt", 0)])
        prog += 6
        yield prog / tot
        for j in range(JC):
            TT(ca_t[:, j, :Tt], ca_t[:, j, :Tt], l1, ALU.mult, [("ca", j), ("lnt", 1)], [("ca", j)])
        for j in range(JC):
            TT(ca_t[:, j, :Tt], ca_t[:, j, :Tt], l0, ALU.add, [("ca", j), ("lnt", 0)], [("ca", j)])
        prog += 8
        yield prog / tot
        for j in range(JC):
            A(actb[s][:, j, :Tt], ca_t[:, j, :Tt], AF.Silu, [("ca", j), ("par",)], [("act", s, j)],
              bias=pcol(l, P_LNB + j), scale=pcol(l, P_LNG + j))
        prog += 4
        yield prog / tot
        wv = wk = None
        for d in range(KC):
            if d % 2 == 0:
                wv, wk = wfetch(st, l, W_OUT0 + d // 2)
            half = d % 2
            bank, bkey = psM()
            korder = [4, 5, 6, 7, 0, 1, 2, 3]
            for ki, k in enumerate(korder):
                MM(bank[:, :Tt], wv[:, k, half * 128:(half + 1) * 128], actb[s][:, k, :Tt], ki == 0, ki == KC - 1,
                   [wk, ("act", s, k)], [bkey])
            TT(xres[xb][:, d, :Tt], bank[:, :Tt], xres[xb][:, d, :Tt], ALU.add, [bkey, ("x", xb, d)], [("x", xb, d)])
            sq_step(st, "M", d)
            prog += 1
            yield prog / tot
        yield 1.0

    def task_F(st, l, last_layer):
        s, Tt, xb = st.s, st.T, st.xb
        if last_layer and st.next is not None and not st.convert:
            x_load(st.next)
        tot = 16.0 + 2 * (FH * 16 + 8 * FH) + (9.0 if last_layer else 0.0)
        prog = 0.0
        rms_from_sq(st, "M", psF)
        prog += 8
        yield prog / tot
        norm_to_act(st, l, P_G2)
        prog += 8
        yield prog / tot
        for hf in range(2):
            for fi in range(FH):
                f = hf * FH + fi
                wv, wk = wfetch(st, l, W_FIN0 + f)
                bg, bgk = psF()
                bv, bvk = psF()
                for k in range(KC):
                    MM(bg[:, :Tt], wv[:, k, 0:128], actb[s][:, k, :Tt], k == 0, k == KC - 1, [wk, ("act", s, k)], [bgk])
                for k in range(KC):
                    MM(bv[:, :Tt], wv[:, k, 128:256], actb[s][:, k, :Tt], k == 0, k == KC - 1, [wk, ("act", s, k)], [bvk])
                sg, sgk = sgR()
                A(sg[:, :Tt], bg[:, :Tt], AF.Silu, [bgk], [sgk])
                TT(hh_t[:, fi, :Tt], sg[:, :Tt], bv[:, :Tt], ALU.mult, [sgk, bvk], [("hh", fi)])
                prog += 16
                yield prog / tot
            for d0 in range(0, KC, 3):
                ds = list(range(d0, min(d0 + 3, KC)))
                defer = (d0 == 0) and not st.convert
                info = []
                for d in ds:
                    wv, wk = wfetch(st, l, W_FOUT0 + hf * 8 + d)
                    bank, bkey = psF()
                    for kk in range(FH - 1 if defer else FH):
                        MM(bank[:, :Tt], wv[:, kk, :], hh_t[:, kk, :Tt], kk == 0, kk == FH - 1,
                           [wk, ("hh", kk)], [bkey])
                    info.append((d, wv, wk, bank, bkey))
                if defer:
                    for (d, wv, wk, bank, bkey) in info:
                        MM(bank[:, :Tt], wv[:, FH - 1, :], hh_t[:, FH - 1, :Tt], False, True,
                           [wk, ("hh", FH - 1)], [bkey])
                for (d, wv, wk, bank, bkey) in info:
                    TT(xres[xb][:, d, :Tt], bank[:, :Tt], xres[xb][:, d, :Tt], ALU.add,
                       [bkey, ("x", xb, d)], [("x", xb, d)])
                    if hf == 1:
                        sq_step(st, "F", d)
                    prog += FH
                yield prog / tot
        if last_layer:
            rms_from_sq(st, "F", psF)
            for k in range(KC):
                STT(xres[xb][:, k, :Tt], xres[xb][:, k, :Tt], par_t[:, L * NP + k:L * NP + k + 1], rstd[s][:, :Tt],
                    ALU.mult, ALU.mult, [("x", xb, k), ("rstd", s), ("par",)], [("x", xb, k)])
            P.dma("pool", ("ys", xb), st.ydst.rearrange("(k p) t -> p k t", p=128), xres[xb][:, :, :Tt],
                  reads=[("x", xb, k) for k in range(KC)])
            prog += 9
            yield prog / tot
        if last_layer and st.convert:
            cv_state["active"] = False
        yield 1.0

    def mk_tile(s, ti, sample=False):
        st = Stream()
        st.s = s
        st.sample = sample
        st.loaded = False
        st.next = None
        st.convert = False
        if sample:
            st.T = TS
            st.first = True
            st.last = True
            st.reset = False
            st.xsrc = xsT
            st.ydst = ysT
            st.pa_dst = lambda l: sa_o[:, l]
            st.pb_dst = lambda l: sb_o[:, l]
            st.ph_dst = lambda l: sh_o[:, l]
        else:
            st.T = T
            st.first = (ti == 0)
            st.last = (ti == NT - 1)
            st.reset = (ti == 0)
            c0 = s * SEQ + ti * T
            st.xsrc = xT[:, c0:c0 + T]
            st.ydst = yT[:, c0:c0 + T]
            st.pa_dst = lambda l: pa_o[:, l, s]
            st.pb_dst = lambda l: pb_o[:, l, s]
            st.ph_dst = lambda l: ph_o[:, l, s]
        return st

    def tasks_of(st):
        out = []
        for l in range(L):
            out.append(task_M(st, l))
            out.append(task_F(st, l, l == L - 1))
        return out

    listA = []
    listB = []
    for ti in range(NT):
        listA.append(mk_tile(0, ti))
        listB.append(mk_tile(1, ti))
    if cfg.get("SAMPLE", True):
        listA.append(mk_tile(0, 0, sample=True))

    order = []
    for ti in range(max(len(listA), len(listB))):
        if ti < len(listA):
            order.append(listA[ti])
        if ti < len(listB):
            order.append(listB[ti])
    for n, st in enumerate(order):
        st.xb = n % NXB
    listA[0].convert = True
    dummy = mk_tile(0, 0)
    dummy.xb = 0
    dummy.convert = True
    P.dry = True
    for l in range(L):
        for _ in task_M(dummy, l):
            pass
        for _ in task_F(dummy, l, l == L - 1):
            pass
    P.dry = False
    cv_state["active"] = True
    for tl in (listA, listB):
        for a, b in zip(tl[:-1], tl[1:]):
            a.next = b

    def stream_gen(tl, offset):
        idx = 0
        for st in tl:
            for l in range(L):
                for fr in task_M(st, l):
                    yield offset + idx + min(fr, 1.0)
                idx += 1
                for fr in task_F(st, l, l == L - 1):
                    yield offset + idx + min(fr, 1.0)
                idx += 1

    gA = stream_gen(listA, 0.0)
    gB = stream_gen(listB, 1.0)
    pA = next(gA, None)
    pB = next(gB, None) if False else 1.0
    startedB = False
    while pA is not None or pB is not None:
        if pB is None or (pA is not None and pA <= pB):
            pA = next(gA, None)
        else:
            pB = next(gB, None)

    P.finish("sp")
    P.emit()
    es.close()
    return nc, P


def _pack_params(inp, L):
    def chunks(v, n):
        return np.ascontiguousarray(v.reshape(n, 128).T)
    par = np.zeros((128, L * NP + 8), np.float32)
    for l in range(L):
        b = l * NP
        par[:, b + P_G1:b + P_G1 + 8] = chunks(inp["norm1_g"][l], 8)
        par[:, b + P_G2:b + P_G2 + 8] = chunks(inp["norm2_g"][l], 8)
        caw = inp["conv_a_w"][l]
        par[:, b + P_CAW:b + P_CAW + 124] = caw.T.reshape(4, 128, CAW).transpose(1, 0, 2).reshape(128, 124)
        par[:, b + P_CAB:b + P_CAB + 4] = chunks(inp["conv_a_b"][l], 4)
        par[:, b + P_LNG:b + P_LNG + 4] = chunks(inp["ln_a_g"][l], 4)
        par[:, b + P_LNB:b + P_LNB + 4] = chunks(inp["ln_a_b"][l], 4)
        cbw = inp["conv_b_w"][l]
        par[:, b + P_CBW:b + P_CBW + 16] = cbw.T.reshape(4, 128, CBW).transpose(1, 0, 2).reshape(128, 16)
        par[:, b + P_CBB:b + P_CBB + 4] = chunks(inp["conv_b_b"][l], 4)
        par[:, b + P_GRB:b + P_GRB + 4] = chunks(inp["gate_r_b"][l], 4)
        par[:, b + P_GIB:b + P_GIB + 4] = chunks(inp["gate_i_b"][l], 4)
        par[:, b + P_LAM:b + P_LAM + 4] = chunks(inp["rglru_lambda"][l], 4)
    par[:, L * NP:L * NP + 8] = chunks(inp["final_norm_g"], 8)
    return par


def _chan_in(a):
    if a.ndim == 2:
        L = a.shape[0]
        return np.ascontiguousarray(a.reshape(L, 4, 128).transpose(2, 0, 1))
    L, n, _ = a.shape
    return np.ascontiguousarray(a.reshape(L, n, 4, 128).transpose(3, 0, 2, 1))


def _chan_out(a):
    if a.ndim == 3:
        L = a.shape[1]
        return np.ascontiguousarray(a.transpose(1, 2, 0).reshape(L, 512))
    L, n = a.shape[1], a.shape[3]
    return np.ascontiguousarray(a.transpose(1, 3, 2, 0).reshape(L, n, 512))


_CACHE = {}


def run(inp, cfg, n_cores=N_CORES):
    L, NT, T, TS = cfg["L"], cfg["NT"], cfg["T"], cfg["TS"]
    SEQ = NT * T
    key = tuple(sorted(cfg.items()))
    if key not in _CACHE:
        _CACHE[key] = build(cfg)[0]
    nc = _CACHE[key]
    f32 = lambda a: np.ascontiguousarray(np.asarray(a, dtype=np.float32))
    inp = {k: f32(v) for k, v in inp.items()}
    par = _pack_params(inp, L)
    in_maps = []
    for c in range(n_cores):
        xp = inp["x_prompt"][2 * c:2 * c + 2]
        xT = np.ascontiguousarray(xp.transpose(2, 0, 1).reshape(D, 2 * SEQ))
        xsT = np.ascontiguousarray(inp["x_sample"][c].T)
        in_maps.append({
            "xT": xT, "xsT": xsT,
            "sca": _chan_in(inp["state_conv_a"][:, c]),
            "scb": _chan_in(inp["state_conv_b"][:, c]),
            "srg": _chan_in(inp["state_rglru"][:, c]),
            "par": par,
            "w_in": inp["w_in"], "w_out": inp["w_out"], "w_fin": inp["w_ffn_in"], "w_fout": inp["w_ffn_out"],
            "gate_r": inp["gate_r_w"], "gate_i": inp["gate_i_w"],
            "ident": np.eye(128, dtype=np.float32),
        })
    res = run_bass_kernel_spmd(nc, in_maps, core_ids=list(range(n_cores)))
    B = 2 * n_cores
    y_prompt = np.zeros((B, SEQ, D), np.float32)
    y_sample = np.zeros((n_cores, TS, D), np.float32)
    pa = np.zeros((L, B, CAW - 1, DC), np.float32)
    pb = np.zeros((L, B, CBW - 1, DC), np.float32)
    ph = np.zeros((L, B, DC), np.float32)
    sa = np.zeros((L, n_cores, CAW - 1, DC), np.float32)
    sbo = np.zeros((L, n_cores, CBW - 1, DC), np.float32)
    sh = np.zeros((L, n_cores, DC), np.float32)
    for c in range(n_cores):
        r = res.results[c]
        y_prompt[2 * c:2 * c + 2] = np.asarray(r["yT"]).reshape(D, 2, SEQ).transpose(1, 2, 0)
        y_sample[c] = np.asarray(r["ysT"]).T
        for s in range(2):
            pa[:, 2 * c + s] = _chan_out(np.asarray(r["pa"])[:, :, s])
            pb[:, 2 * c + s] = _chan_out(np.asarray(r["pb"])[:, :, s])
            ph[:, 2 * c + s] = _chan_out(np.asarray(r["ph"])[:, :, s])
        sa[:, c] = _chan_out(np.asarray(r["sa"]))
        sbo[:, c] = _chan_out(np.asarray(r["sb"]))
        sh[:, c] = _chan_out(np.asarray(r["sh"]))
    return (y_prompt, y_sample, pa, pb, ph, sa, sbo, sh)


FULL_CFG = {"L": 4, "NT": 8, "T": 512, "TS": 64}


def kernel(**inputs):
    return run(inputs, FULL_CFG)
```

```python
import contextlib
import numpy as np
import concourse.bass as bass
import concourse.mybir as mybir
from concourse.bass_utils import run_bass_kernel_spmd

F32 = mybir.dt.float32
BF16 = mybir.dt.bfloat16
AF = mybir.ActivationFunctionType
ALU = mybir.AluOpType

D = 1024
KC = 8
DC = 512
JC = 4
DFF = 2816
FC = 22
FH = 11
CAW = 31
CBW = 4
RMS_EPS = 1e-6
LN_EPS = 1e-5
N_CORES = 8

P_G1, P_G2, P_CAW, P_CAB, P_LNG, P_LNB, P_CBW, P_CBB, P_GRB, P_GIB, P_LAM = (
    0, 8, 16, 140, 144, 148, 152, 168, 172, 176, 180)
NP = 184
Q_HRB, Q_HIB, Q_CL, Q_HCL = 0, 4, 8, 12
NQ = 16
W_IN0, W_OUT0, W_FIN0, W_FOUT0, W_DG0, W_PER_L = 0, 8, 12, 34, 50, 52
NPE = 5
WT_COLS = 2048


class Prog:
    ENGS = ("pe", "act", "dve", "pool", "sp")

    def __init__(self, nc, es):
        self.nc = nc
        self.es = es
        self.streams = {e: [] for e in self.ENGS}
        self.sems = {e: es.enter_context(nc.semaphore("s_" + e)) for e in self.ENGS}
        self.cnt = {e: 0 for e in self.ENGS}
        self.known = {e: {} for e in self.ENGS}
        self.res = {}
        self.chan = {}
        self.semobj = {}
        for e in self.ENGS:
            self.semobj[id(self.sems[e])] = self.sems[e]
        self.own = {e: id(self.sems[e]) for e in self.ENGS}
        self.nwaits = 0
        self.dry = False

    def _waits(self, eng, toks):
        need = {}
        for (sid, val) in toks:
            if need.get(sid, 0) < val:
                need[sid] = val
        for sid, val in need.items():
            if sid == self.own[eng]:
                if eng in ("pe", "sp"):
                    continue
            if self.known[eng].get(sid, 0) >= val:
                continue
            self.known[eng][sid] = val
            sem = self.semobj[sid]
            self.streams[eng].append(("w", sem, val))
            self.nwaits += 1

    def _deps(self, reads, writes):
        toks = []
        for k in reads:
            r = self.res.get(k)
            if r is not None and r[0] is not None:
                toks.append(r[0])
        for k in writes:
            r = self.res.get(k)
            if r is not None:
                if r[0] is not None:
                    toks.append(r[0])
                toks.extend(r[1].values())
        return toks

    def _record(self, reads, writes, tok):
        for k in reads:
            r = self.res.get(k)
            if r is None:
                r = [None, {}]
                self.res[k] = r
            old = r[1].get(tok[0])
            if old is None or old[1] < tok[1]:
                r[1][tok[0]] = tok
        for k in writes:
            self.res[k] = [tok, {}]

    def op(self, eng, fn, reads=(), writes=(), inc=True):
        if self.dry:
            return
        self._waits(eng, self._deps(reads, writes))
        if inc:
            self.cnt[eng] += 1
            tok = (self.own[eng], self.cnt[eng])
            self.streams[eng].append(("i", fn, self.sems[eng]))
        else:
            tok = (self.own[eng], self.cnt[eng] + 1)
            self.streams[eng].append(("n", fn))
        self._record(reads, writes, tok)

    def dma(self, q, chan, out, in_, reads=(), writes=()):
        if self.dry:
            return
        c = self.chan.get(chan)
        if c is None:
            sem = self.es.enter_context(self.nc.semaphore("d_" + str(len(self.chan))))
            self.semobj[id(sem)] = sem
            c = [sem, 0]
            self.chan[chan] = c
        toks = self._deps(reads, writes)
        if c[1] > 0:
            toks.append((id(c[0]), c[1]))
        self._waits(q, toks)
        c[1] += 16
        tok = (id(c[0]), c[1])
        self.streams[q].append(("d", out, in_, c[0]))
        self._record(reads, writes, tok)

    def finish(self, q="sp"):
        toks = [(id(c[0]), c[1]) for c in self.chan.values() if c[1] > 0]
        self._waits(q, toks)

    def emit(self):
        nc = self.nc
        block = self.es.enter_context(nc.Block())
        engmap = {"pe": block.tensor, "act": block.scalar, "dve": block.vector,
                  "pool": block.gpsimd, "sp": block.sync}
        for ename in self.ENGS:
            items = self.streams[ename]

            def body(e, items=items):
                for it in items:
                    k = it[0]
                    if k == "w":
                        e.wait_ge(it[1], it[2])
                    elif k == "i":
                        it[1](e).then_inc(it[2], 1)
                    elif k == "n":
                        it[1](e)
                    else:
                        e.dma_start(out=it[1], in_=it[2]).then_inc(it[3], 16)
            engmap[ename](body)


class Rot:
    def __init__(self, items):
        self.items = items
        self.i = 0

    def __call__(self):
        it = self.items[self.i % len(self.items)]
        self.i += 1
        return it


class Stream:
    pass


def build(cfg):
    L = cfg["L"]
    NT = cfg["NT"]
    T = cfg["T"]
    TS = cfg["TS"]
    SEQ = NT * T
    NW = L * W_PER_L
    NRING = cfg.get("NRING", 7)

    nc = bass.Bass("TRN2", target_bir_lowering=False)
    es = contextlib.ExitStack()

    def din(name, shape, dt=F32):
        return nc.dram_tensor(name, list(shape), dt, kind="ExternalInput").ap()

    def dout(name, shape):
        return nc.dram_tensor(name, list(shape), F32, kind="ExternalOutput").ap()

    xT = din("xT", [D, 2 * SEQ])
    xsT = din("xsT", [D, TS])
    sca_in = din("sca", [128, L, JC, CAW - 1])
    scb_in = din("scb", [128, L, JC, CBW - 1])
    srg_in = din("srg", [128, L, JC])
    par_in = din("par", [128, L * NP + 8])
    w_in = din("w_in", [L, D, 4 * DC])
    w_out = din("w_out", [L, D, D])
    w_fin = din("w_fin", [L, D, 2 * DFF])
    w_fout = din("w_fout", [L, DFF, D])
    gr_in = din("gate_r", [L, 8, 64, 64])
    gi_in = din("gate_i", [L, 8, 64, 64])
    ident_in = din("ident", [128, 128])

    yT = dout("yT", [D, 2 * SEQ])
    ysT = dout("ysT", [D, TS])
    pa_o = dout("pa", [128, L, 2, JC, CAW - 1])
    pb_o = dout("pb", [128, L, 2, JC, CBW - 1])
    ph_o = dout("ph", [128, L, 2, JC])
    sa_o = dout("sa", [128, L, JC, CAW - 1])
    sb_o = dout("sb", [128, L, JC, CBW - 1])
    sh_o = dout("sh", [128, L, JC])

    wsc = nc.dram_tensor("wsc", [NW, 128, WT_COLS], BF16).ap()

    def sb(name, shape, dt=F32):
        return es.enter_context(nc.sbuf_tensor("sb_" + name, list(shape), dt))

    P = Prog(nc, es)

    NXB = 3
    xres = [sb("xres%d" % s, [128, KC, T]) for s in range(NXB)]
    actb = [sb("actb%d" % s, [128, KC, T], BF16) for s in range(2)]
    rstd = [sb("rstd%d" % s, [128, T]) for s in range(2)]
    ust = [sb("ust%d" % s, [128, L, JC, CAW - 1]) for s in range(2)]
    xbst = [sb("xbst%d" % s, [128, L, JC, CBW - 1]) for s in range(2)]
    hst = [sb("hst%d" % s, [128, L, JC]) for s in range(2)]
    u_t = sb("u", [128, JC, CAW - 1 + T])
    ca_t = sb("ca", [128, JC, T])
    cab_t = sb("cab", [128, 2 * JC, T], BF16)
    xb_t = sb("xb", [128, JC, CBW - 1 + T])
    gb_t = sb("gb", [128, JC, T], BF16)
    xcb_t = sb("xcb", [128, 2, T])
    xcbb_t = sb("xcbb", [128, 2, T], BF16)
    P_t = sb("Pt", [128, 2, T])
    Q_t = sb("Qt", [128, 2, T])
    R_t = sb("Rt", [128, 2, T])
    ubf_t = sb("ubf", [128, JC, CAW - 1 + T], BF16)
    ident_t = sb("ident", [128, 128])
    lnt_t = sb("lnt", [128, 2, T])
    sq8 = {"M": sb("sqM", [128, KC, T], BF16), "F": sb("sqF", [128, KC, T], BF16)}
    hh_t = sb("hh", [128, FH, T], BF16)
    sg_t = sb("sg", [128, 2, T])
    ring_t = sb("ring", [128, NRING, WT_COLS], BF16)
    par_t = sb("par", [128, L * NP + 8])
    dpar_t = sb("dpar", [128, L, NQ])
    bdg_t = sb("bdg", [128, L * 8, 128], BF16)
    ones_t = sb("ones", [128, 128], BF16)
    psum = [es.enter_context(nc.psum_tensor("ps%d" % i, [128, 512], F32)) for i in range(8)]

    psM = Rot([(psum[i], ("ps", i)) for i in range(0, 3)])
    psF = Rot([(psum[i], ("ps", i)) for i in range(3, 8)])
    thR = Rot([(lnt_t[:, i, :], ("lnt", i)) for i in range(2)])
    sgR = Rot([(sg_t[:, i, :], ("sg", i)) for i in range(2)])
    ring_state = {"i": 0}

    def pcol(l, c0, n=1):
        return par_t[:, l * NP + c0: l * NP + c0 + n]

    def qcol(l, c0, n=1):
        return dpar_t[:, l, c0:c0 + n]

    def A(eng_out, in_, func, reads, writes, bias=0.0, scale=1.0):
        P.op("act", lambda e: e.activation(out=eng_out, in_=in_, func=func, bias=bias, scale=scale),
             reads, writes)

    def MM(out, lhsT, rhs, start, stop, reads, writes, inc=None):
        P.op("pe", lambda e: e.matmul(out, lhsT, rhs, start=start, stop=stop), reads, writes,
             inc=(stop if inc is None else inc))

    def STT(out, in0, scalar, in1, op0, op1, reads, writes):
        P.op("dve", lambda e: e.scalar_tensor_tensor(out=out, in0=in0, scalar=scalar, in1=in1, op0=op0, op1=op1),
             reads, writes)

    def TT(out, in0, in1, op, reads, writes):
        P.op("dve", lambda e: e.tensor_tensor(out=out, in0=in0, in1=in1, op=op), reads, writes)

    def TS2(out, in0, s1, s2, op0, op1, reads, writes, eng="dve"):
        P.op(eng, lambda e: e.tensor_scalar(out=out, in0=in0, scalar1=s1, scalar2=s2, op0=op0, op1=op1),
             reads, writes)

    P.dma("sp", "par", par_t[:], par_in, writes=[("par",)])
    P.op("dve", lambda e: e.memset(ones_t[:], 1.0), writes=[("ones",)])
    P.dma("sp", "ident", ident_t[:], ident_in, writes=[("ident",)])
    for l in range(L):
        A(qcol(l, Q_CL, 4), pcol(l, P_LAM, 4), AF.Exp, [("par",)], [("dpar", l)], scale=-1.0)
        A(qcol(l, Q_CL, 4), qcol(l, Q_CL, 4), AF.Ln, [("dpar", l)], [("dpar", l)], bias=1.0)
        P.op("act", lambda e, l=l: e.mul(qcol(l, Q_HCL, 4), qcol(l, Q_CL, 4), -4.0), [("dpar", l)], [("dpar", l)])
        P.op("act", lambda e, l=l: e.mul(qcol(l, Q_CL, 4), qcol(l, Q_CL, 4), -8.0), [("dpar", l)], [("dpar", l)])
        P.op("act", lambda e, l=l: e.mul(qcol(l, Q_HRB, 4), pcol(l, P_GRB, 4), 0.5), [("par",)], [("dpar", l)])
        P.op("act", lambda e, l=l: e.mul(qcol(l, Q_HIB, 4), pcol(l, P_GIB, 4), 0.5), [("par",)], [("dpar", l)])
    assert L * 8 * 128 <= KC * T
    xk1 = [("x", 1, k) for k in range(KC)]
    stage_flat = xres[1][:].rearrange("p k t -> p (k t)")
    bd_stage = stage_flat[:, 0:L * 8 * 128].rearrange("p (m q) -> p m q", q=128)
    P.op("dve", lambda e: e.memset(bd_stage, 0.0), writes=xk1)
    for l in range(L):
        for g, gin in enumerate((gr_in, gi_in)):
            v = gin[l].rearrange("(c two) i j -> two i c j", two=2)
            m0 = (l * 2 + g) * 4
            P.dma("sp", ("bd", (l * 2 + g) % 4, 0), bd_stage[0:64, m0:m0 + 4, 0:64], v[0], writes=xk1)
            P.dma("sp", ("bd", (l * 2 + g) % 4, 1), bd_stage[64:128, m0:m0 + 4, 64:128], v[1], writes=xk1)
    P.op("dve", lambda e: e.tensor_copy(out=bdg_t[:], in_=bd_stage), reads=xk1, writes=[("bdg",)])

    def wsrc(l, t):
        if t < W_OUT0:
            i = t - W_IN0
            sec, j = (0, i) if i < 4 else (2, i - 4)
            v = w_in[l].rearrange("(k p) (s j q) -> p k s j q", p=128, s=4, j=4, q=128)
            return v[:, :, sec:sec + 2, j, :], 2048, ("inA" if i < 4 else "inB")
        if t < W_FIN0:
            i = t - W_OUT0
            v = w_out[l].rearrange("(k p) c -> p k c", p=128)
            return v[:, :, 256 * i:256 * i + 256], 2048, "out"
        if t < W_FOUT0:
            f = t - W_FIN0
            v = w_fin[l].rearrange("(k p) (s f q) -> p k s f q", p=128, s=2, f=FC, q=128)
            return v[:, :, :, f, :], 2048, "fin"
        i = t - W_FOUT0
        hf, d = i // 8, i % 8
        v = w_fout[l].rearrange("(h kk p) (d q) -> p h kk d q", h=2, kk=FH, p=128, d=8, q=128)
        return v[:, hf, :, d, :], FH * 128, "fout"

    cv_flat = xres[2][:].rearrange("p k t -> p (k t)")
    assert KC * T >= 4096
    cv_stage = []
    for h in range(2):
        keys = [("x", 2, k) for k in range(KC) if (k * T < (h + 1) * 2048 and (k + 1) * T > h * 2048)]
        cv_stage.append((cv_flat[:, h * 2048:(h + 1) * 2048], keys))
    cv_stage.append((ring_t[:, NRING - 2:NRING, :].rearrange("p a b -> p (a b)").bitcast(F32),
                     [("w", NRING - 2), ("w", NRING - 1)]))
    NCV = len(cv_stage)
    cv_state = {"n": 0, "active": True}

    cv_seq = []
    cv_slot = {}
    cv_state.update({"loaded": 0, "cast": 0})
    LOADAHEAD, CASTAHEAD = 2, 1

    def tile_view(slot, t):
        if t >= W_DG0:
            return ring_t[:, slot, 0:2 * NPE * 128].rearrange("p (a b) -> p a b", b=128)
        if t >= W_FOUT0:
            return ring_t[:, slot, 0:FH * 128].rearrange("p (a b) -> p a b", b=128)
        return ring_t[:, slot, :].rearrange("p (a b) -> p a b", b=256)

    def next_slot():
        slot = ring_state["i"] % (NRING - 4 if cv_state["active"] else NRING)
        ring_state["i"] += 1
        return slot

    def cv_issue_load(i):
        l, t = cv_seq[i]
        if t >= W_DG0:
            return
        src, ncols, kind = wsrc(l, t)
        stg, keys = cv_stage[i % NCV]
        if kind == "fout":
            P.dma("sp", ("pl", i % NCV, 0), stg[:, 0:ncols].rearrange("p (a b) -> p a b", b=128), src, writes=keys)
        elif kind == "out":
            P.dma("sp", ("pl", i % NCV, 0), stg.rearrange("p (a b) -> p a b", b=256), src, writes=keys)
        else:
            dst = stg.rearrange("p (a s b) -> p a s b", s=2, b=128)
            P.dma("sp", ("pl", i % NCV, 0), dst[:, :, 0, :], src[:, :, 0, :], writes=keys)
            P.dma("sp", ("pl", i % NCV, 1), dst[:, :, 1, :], src[:, :, 1, :], writes=keys)

    def cv_issue_cast(i):
        l, t = cv_seq[i]
        slot = NRING - 4 + (i % 2)
        cv_slot[i] = slot
        wkey = ("w", slot)
        if t >= W_DG0:
            ci = t - W_DG0
            for jj in range(2):
                j = 2 * ci + jj
                for k in range(NPE):
                    m = jj * NPE + k
                    TS2(ring_t[:, slot, m * 128:(m + 1) * 128], ident_t[:], pcol(l, P_CAW + j * CAW + k), None,
                        ALU.mult, ALU.bypass, [("ident",), ("par",)], [wkey])
            ncols = 2 * NPE * 128
        else:
            src, ncols, kind = wsrc(l, t)
            stg, keys = cv_stage[i % NCV]
            if kind == "inA":
                sv = stg.rearrange("p (a s b) -> p a s b", s=2, b=128)
                dv = ring_t[:, slot, :].rearrange("p (a s b) -> p a s b", s=2, b=128)
                TS2(dv[:, :, 0, :], sv[:, :, 0, :], 0.5, None, ALU.mult, ALU.bypass, keys, [wkey])
                P.op("dve", lambda e: e.tensor_copy(out=dv[:, :, 1, :], in_=sv[:, :, 1, :]), keys, [wkey])
            else:
                o = ring_t[:, slot, 0:ncols]
                i_ = stg[:, 0:ncols]
                P.op("dve", lambda e: e.tensor_copy(out=o, in_=i_), keys, [wkey])
        P.dma("pool", ("pst", slot), wsc[l * W_PER_L + t][:, 0:ncols], ring_t[:, slot, 0:ncols],
              reads=[wkey], writes=[("wsc", l, t)])

    def wfetch(st, l, t):
        if P.dry:
            if st.convert:
                cv_seq.append((l, t))
            return tile_view(0, t), ("w", 0)
        if st.convert:
            n = cv_state["n"]
            cv_state["n"] += 1
            assert cv_seq[n] == (l, t), (n, cv_seq[n], l, t)
            while cv_state["loaded"] < min(n + LOADAHEAD + 1, len(cv_seq)):
                cv_issue_load(cv_state["loaded"])
                cv_state["loaded"] += 1
            while cv_state["cast"] < min(n + CASTAHEAD + 1, len(cv_seq)):
                cv_issue_cast(cv_state["cast"])
                cv_state["cast"] += 1
            slot = cv_slot[n]
            return tile_view(slot, t), ("w", slot)
        slot = next_slot()
        wkey = ("w", slot)
        ncols = FH * 128 if W_FOUT0 <= t < W_DG0 else (2 * NPE * 128 if t >= W_DG0 else 2048)
        P.dma("sp", ("ring", slot), ring_t[:, slot, 0:ncols], wsc[l * W_PER_L + t][:, 0:ncols],
              reads=[("wsc", l, t)], writes=[wkey])
        return tile_view(slot, t), wkey

    def sq_step(st, which, k):
        Tt, xb = st.T, st.xb
        buf = sq8[which]
        A(buf[:, k, :Tt], xres[xb][:, k, :Tt], AF.Square, [("x", xb, k)], [("sq", which, k)])

    def rms_from_sq(st, which, psR):
        s, Tt = st.s, st.T
        bank, bkey = psR()
        buf = sq8[which]
        for k in range(KC):
            MM(bank[:, :Tt], ones_t[:], buf[:, k, :Tt], k == 0, k == KC - 1, [("sq", which, k), ("ones",)], [bkey])
        A(rstd[s][:, :Tt], bank[:, :Tt], AF.Ln, [bkey], [("rstd", s)], bias=RMS_EPS, scale=1.0 / D)
        A(rstd[s][:, :Tt], rstd[s][:, :Tt], AF.Exp, [("rstd", s)], [("rstd", s)], scale=-0.5)

    def norm_to_act(st, l, gcol):
        s, Tt, xb = st.s, st.T, st.xb
        for k in range(KC):
            STT(actb[s][:, k, :Tt], xres[xb][:, k, :Tt], pcol(l, gcol + k), rstd[s][:, :Tt], ALU.mult, ALU.mult,
                [("x", xb, k), ("rstd", s), ("par",)], [("act", s, k)])

    def x_load(st):
        Tt, xb = st.T, st.xb
        P.dma("pool", ("xl", xb), xres[xb][:, :, :Tt],
              st.xsrc.rearrange("(k p) t -> p k t", p=128), writes=[("x", xb, k) for k in range(KC)])
        st.loaded = True

    def tile_prologue(st):
        s, Tt = st.s, st.T
        if not st.loaded:
            x_load(st)
        if st.first:
            if st.sample:
                P.dma("pool", ("sti", 0), ust[s][:], sca_in, writes=[("ust", s, l) for l in range(L)])
                P.dma("pool", ("sti", 1), xbst[s][:], scb_in, writes=[("xbst", s, l) for l in range(L)])
                P.dma("pool", ("sti", 2), hst[s][:], srg_in, writes=[("hst", s, l) for l in range(L)])
            else:
                P.op("act", lambda e: e.memzero(ust[s][:]), writes=[("ust", s, l) for l in range(L)])
                P.op("act", lambda e: e.memzero(xbst[s][:]), writes=[("xbst", s, l) for l in range(L)])

    def task_M(st, l):
        s, Tt, xb = st.s, st.T, st.xb
        tot = 186.0
        prog = 0.0
        if l == 0:
            tile_prologue(st)
            for k in range(KC):
                sq_step(st, "F", k)
        P.op("act", lambda e: e.copy(u_t[:, :, 0:CAW - 1], ust[s][:, l]), [("ust", s, l)], [("uh",)])
        P.op("act", lambda e: e.copy(ubf_t[:, :, 0:CAW - 1], ust[s][:, l]), [("ust", s, l)], [("ubh",)])
        P.op("act", lambda e: e.copy(xb_t[:, :, 0:CBW - 1], xbst[s][:, l]), [("xbst", s, l)], [("xbh",)])
        rms_from_sq(st, "F", psM)
        prog += 3
        yield prog / tot
        norm_to_act(st, l, P_G1)
        prog += 8
        yield prog / tot

        def pairA(j):
            wv, wk = wfetch(st, l, W_IN0 + j)
            bv, bvk = psM()
            bg, bgk = psM()
            for k in range(KC):
                MM(bv[:, :Tt], wv[:, k, 0:128], actb[s][:, k, :Tt], k == 0, k == KC - 1, [wk, ("act", s, k)], [bvk])
            for k in range(KC):
                MM(bg[:, :Tt], wv[:, k, 128:256], actb[s][:, k, :Tt], k == 0, k == KC - 1, [wk, ("act", s, k)], [bgk])
            th, thk = thR()
            A(th[:, :Tt], bg[:, :Tt], AF.Tanh, [bgk], [thk], scale=0.5)
            STT(u_t[:, j, CAW - 1:CAW - 1 + Tt], th[:, :Tt], 1.0, bv[:, :Tt], ALU.add, ALU.mult,
                [thk, bvk], [("u", j)])

        def pairB(j):
            wv, wk = wfetch(st, l, W_IN0 + 4 + j)
            bx, bxk = psM()
            bt, btk = psM()
            for k in range(KC):
                MM(bx[:, :Tt], wv[:, k, 0:128], actb[s][:, k, :Tt], k == 0, k == KC - 1, [wk, ("act", s, k)], [bxk])
            for k in range(KC):
                MM(bt[:, :Tt], wv[:, k, 128:256], actb[s][:, k, :Tt], k == 0, k == KC - 1, [wk, ("act", s, k)], [btk])
            P.op("act", lambda e: e.copy(xb_t[:, j, CBW - 1:CBW - 1 + Tt], bx[:, :Tt]), [bxk], [("xb", j)])
            A(gb_t[:, j, :Tt], bt[:, :Tt], AF.Gelu_apprx_tanh, [btk], [("gb", j)])

        def convA(js):
            for j in js:
                P.op("act", lambda e, j=j: e.copy(ubf_t[:, j, CAW - 1:CAW - 1 + Tt], u_t[:, j, CAW - 1:CAW - 1 + Tt]),
                     [("u", j)], [("ubf", j)])
            yield 0
            dv, dk = wfetch(st, l, W_DG0 + js[0] // 2)
            for jj, j in enumerate(js):
                bank, bkey = psM()
                for k in range(NPE):
                    MM(bank[:, :Tt], dv[:, jj * NPE + k, :], ubf_t[:, j, k:k + Tt], k == 0, k == NPE - 1,
                       [dk, ("ubf", j), ("ubh",)], [bkey])
                TS2(ca_t[:, j, :Tt], bank[:, :Tt], pcol(l, P_CAB + j), None, ALU.add, ALU.bypass,
                    [bkey, ("par",)], [("ca", j)])
            yield 2
            for k in range(NPE, CAW):
                for j in js:
                    src = u_t[:, j, k:k + Tt]
                    w = pcol(l, P_CAW + j * CAW + k)
                    STT(ca_t[:, j, :Tt], src, w, ca_t[:, j, :Tt], ALU.mult, ALU.add,
                        [("u", j), ("uh",), ("ca", j)], [("ca", j)])
                yield 2

        def chainB(gq):
            js = (2 * gq, 2 * gq + 1)
            for kk in range(CBW):
                for j in js:
                    q = j % 2
                    src = xb_t[:, j, kk:kk + Tt]
                    w = pcol(l, P_CBW + j * CBW + kk)
                    if kk == 0:
                        TS2(xcb_t[:, q, :Tt], src, w, pcol(l, P_CBB + j), ALU.mult, ALU.add,
                            [("xb", j), ("xbh",), ("par",)], [("xcb", q)])
                    else:
                        STT(xcb_t[:, q, :Tt], src, w, xcb_t[:, q, :Tt], ALU.mult, ALU.add,
                            [("xb", j), ("xbh",), ("xcb", q)], [("xcb", q)])
            for j in js:
                q = j % 2
                P.op("dve", lambda e, q=q: e.tensor_copy(out=xcbb_t[:, q, :Tt], in_=xcb_t[:, q, :Tt]),
                     [("xcb", q)], [("xcbb", q)])
            yield 9
            yield 0
            for j in js:
                q = j % 2
                pr, prk = psM()
                pi, pik = psM()
                MM(pr[:, :Tt], bdg_t[:, (l * 2 + 0) * 4 + j, :], xcbb_t[:, q, :Tt], True, True,
                   [("bdg",), ("xcbb", q)], [prk])
                MM(pi[:, :Tt], bdg_t[:, (l * 2 + 1) * 4 + j, :], xcbb_t[:, q, :Tt], True, True,
                   [("bdg",), ("xcbb", q)], [pik])
                A(P_t[:, q, :Tt], pr[:, :Tt], AF.Tanh, [prk, ("dpar", l)], [("P", q)],
                  bias=qcol(l, Q_HRB + j), scale=0.5)
                A(Q_t[:, q, :Tt], pi[:, :Tt], AF.Tanh, [pik, ("dpar", l)], [("Q", q)],
                  bias=qcol(l, Q_HIB + j), scale=0.5)
            yield 0
            for j in js:
                q = j % 2
                A(R_t[:, q, :Tt], P_t[:, q, :Tt], AF.Exp, [("P", q)], [("R", q)],
                  bias=qcol(l, Q_HCL + j), scale=qcol(l, Q_HCL + j))
                A(P_t[:, q, :Tt], P_t[:, q, :Tt], AF.Exp, [("P", q)], [("P", q)],
                  bias=qcol(l, Q_CL + j), scale=qcol(l, Q_CL + j))
            yield 0
            for j in js:
                q = j % 2
                A(P_t[:, q, :Tt], P_t[:, q, :Tt], AF.Relu, [("P", q)], [("P", q)], bias=1.0, scale=-1.0)
            for j in js:
                q = j % 2
                A(P_t[:, q, :Tt], P_t[:, q, :Tt], AF.Ln, [("P", q)], [("P", q)], bias=1e-18)
                A(P_t[:, q, :Tt], P_t[:, q, :Tt], AF.Exp, [("P", q)], [("P", q)], bias=float(np.log(0.5)), scale=0.5)
            yield 0
            yield 0
            for j in js:
                q = j % 2
                STT(Q_t[:, q, :Tt], Q_t[:, q, :Tt], 1.0, xcb_t[:, q, :Tt], ALU.add, ALU.mult,
                    [("Q", q), ("xcb", q)], [("Q", q)])
            if st.reset:
                for j in js:
                    q = j % 2
                    P.op("dve", lambda e, q=q: e.memset(P_t[:, q, 0:1], 0.5), [("P", q)], [("P", q)])
            for j in js:
                q = j % 2
                TT(Q_t[:, q, :Tt], Q_t[:, q, :Tt], P_t[:, q, :Tt], ALU.mult, [("Q", q), ("P", q)], [("Q", q)])
            for j in js:
                q = j % 2
                init = 0.0 if st.reset else hst[s][:, l, j:j + 1]
                P.op("dve", lambda e, q=q, init=init: e.tensor_tensor_scan(
                    out=P_t[:, q, :Tt], data0=R_t[:, q, :Tt], data1=Q_t[:, q, :Tt], initial=init,
                    op0=ALU.mult, op1=ALU.add), [("R", q), ("Q", q), ("hst", s, l)], [("P", q)])
            for j in js:
                q = j % 2
                P.op("act", lambda e, q=q, j=j: e.copy(hst[s][:, l, j:j + 1], P_t[:, q, Tt - 1:Tt]),
                     [("P", q)], [("hst", s, l)])
                TT(actb[s][:, JC + j, :Tt], P_t[:, q, :Tt], gb_t[:, j, :Tt], ALU.mult,
                   [("P", q), ("gb", j)], [("act", s, JC + j)])
            yield 10

        def side0():
            pairB(0); yield 0.5
            pairB(1); yield 0.5
            gb0 = chainB(0)
            yield next(gb0)
            pairA(2); yield 1
            yield next(gb0)
            yield next(gb0)
            pairA(3); yield 1
            yield next(gb0)
            pairB(2); yield 0.5
            yield next(gb0)
            pairB(3); yield 0.5
            yield next(gb0)
            for w_ in gb0:
                yield w_

        def interleave(main, side, every):
            nonlocal prog
            n = 0
            sdone = False
            for wgt in main:
                prog += wgt
                n += 1
                if not sdone and n % every == 0:
                    try:
                        prog += next(side)
                    except StopIteration:
                        sdone = True
                yield prog / tot
            while not sdone:
                try:
                    prog += next(side)
                    yield prog / tot
                except StopIteration:
                    sdone = True

        pairA(0)
        prog += 1
        yield prog / tot
        pairA(1)
        prog += 1
        yield prog / tot
        for fr in interleave(convA((0, 1)), side0(), 2):
            yield fr
        def cab_ops(js):
            for j in js:
                P.op("act", lambda e, j=j: e.copy(cab_t[:, j, :Tt], ca_t[:, j, :Tt]), [("ca", j)], [("cab", j)])
                A(cab_t[:, JC + j, :Tt], ca_t[:, j, :Tt], AF.Square, [("ca", j)], [("cab", JC + j)])

        def side1():
            yield 0
            cab_ops((0, 1))
            for w_ in chainB(1):
                yield w_

        for fr in interleave(convA((2, 3)), side1(), 3):
            yield fr
        P.op("act", lambda e: e.copy(xbst[s][:, l], xb_t[:, :, Tt:Tt + CBW - 1]),
             [("xb", j) for j in range(JC)] + [("xbh",)], [("xbst", s, l)])
        if st.last:
            P.dma("pool", ("sto", 1), st.pb_dst(l), xbst[s][:, l], reads=[("xbst", s, l)])
            P.dma("pool", ("sto", 2), st.ph_dst(l), hst[s][:, l], reads=[("hst", s, l)])
        P.op("act", lambda e: e.copy(ust[s][:, l], u_t[:, :, Tt:Tt + CAW - 1]),
             [("u", j) for j in range(JC)] + [("uh",)], [("ust", s, l)])
        if st.last:
            P.dma("pool", ("sto", 0), st.pa_dst(l), ust[s][:, l], reads=[("ust", s, l)])
        cab_ops((2, 3))
        prog += 4
        yield prog / tot
        s1, s1k = psM()
        s2, s2k = psM()
        for j in range(JC):
            MM(s1[:, :Tt], ones_t[:], cab_t[:, j, :Tt], j == 0, j == JC - 1, [("cab", j), ("ones",)], [s1k])
        for j in range(JC):
            MM(s2[:, :Tt], ones_t[:], cab_t[:, JC + j, :Tt], j == 0, j == JC - 1, [("cab", JC + j), ("ones",)], [s2k])
        l0 = lnt_t[:, 0, :Tt]
        l1 = lnt_t[:, 1, :Tt]
        A(l0, s1[:, :Tt], AF.Square, [s1k], [("lnt", 0)], scale=1.0 / DC)
        STT(l1, s2[:, :Tt], 1.0 / DC, l0, ALU.mult, ALU.subtract, [s2k, ("lnt", 0)], [("lnt", 1)])
        A(l1, l1, AF.Relu, [("lnt", 1)], [("lnt", 1)])
        A(l1, l1, AF.Ln, [("lnt", 1)], [("lnt", 1)], bias=LN_EPS)
        A(l1, l1, AF.Exp, [("lnt", 1)], [("lnt", 1)], scale=-0.5)
        STT(l0, s1[:, :Tt], -1.0 / DC, l1, ALU.mult, ALU.mult, [s1k, ("lnt", 1), ("lnt", 0)], [("lnt", 0)])
        prog += 6
        yield prog / tot
        for j in range(JC):
            TT(ca_t[:, j, :Tt], ca_t[:, j, :Tt], l1, ALU.mult, [("ca", j), ("lnt", 1)], [("ca", j)])
        for j in range(JC):
            TT(ca_t[:, j, :Tt], ca_t[:, j, :Tt], l0, ALU.add, [("ca", j), ("lnt", 0)], [("ca", j)])
        prog += 8
        yield prog / tot
        for j in range(JC):
            A(actb[s][:, j, :Tt], ca_t[:, j, :Tt], AF.Silu, [("ca", j), ("par",)], [("act", s, j)],
              bias=pcol(l, P_LNB + j), scale=pcol(l, P_LNG + j))
        prog += 4
        yield prog / tot
        wv = wk = None
        for d in range(KC):
            if d % 2 == 0:
                wv, wk = wfetch(st, l, W_OUT0 + d // 2)
            half = d % 2
            bank, bkey = psM()
            korder = [4, 5, 6, 7, 0, 1, 2, 3]
            for ki, k in enumerate(korder):
                MM(bank[:, :Tt], wv[:, k, half * 128:(half + 1) * 128], actb[s][:, k, :Tt], ki == 0, ki == KC - 1,
                   [wk, ("act", s, k)], [bkey])
            TT(xres[xb][:, d, :Tt], bank[:, :Tt], xres[xb][:, d, :Tt], ALU.add, [bkey, ("x", xb, d)], [("x", xb, d)])
            sq_step(st, "M", d)
            prog += 1
            yield prog / tot
        yield 1.0

    def task_F(st, l, last_layer):
        s, Tt, xb = st.s, st.T, st.xb
        if last_layer and st.next is not None and not st.convert:
            x_load(st.next)
        tot = 16.0 + 2 * (FH * 16 + 8 * FH) + (9.0 if last_layer else 0.0)
        prog = 0.0
        rms_from_sq(st, "M", psF)
        prog += 8
        yield prog / tot
        norm_to_act(st, l, P_G2)
        prog += 8
        yield prog / tot
        for hf in range(2):
            for fi in range(FH):
                f = hf * FH + fi
                wv, wk = wfetch(st, l, W_FIN0 + f)
                bg, bgk = psF()
                bv, bvk = psF()
                for k in range(KC):
                    MM(bg[:, :Tt], wv[:, k, 0:128], actb[s][:, k, :Tt], k == 0, k == KC - 1, [wk, ("act", s, k)], [bgk])
                for k in range(KC):
                    MM(bv[:, :Tt], wv[:, k, 128:256], actb[s][:, k, :Tt], k == 0, k == KC - 1, [wk, ("act", s, k)], [bvk])
                sg, sgk = sgR()
                A(sg[:, :Tt], bg[:, :Tt], AF.Silu, [bgk], [sgk])
                TT(hh_t[:, fi, :Tt], sg[:, :Tt], bv[:, :Tt], ALU.mult, [sgk, bvk], [("hh", fi)])
                prog += 16
                yield prog / tot
            for ds in ([0, 1], [2], [3], [4], [5], [6], [7]):
                d0 = ds[0]
                defer = (d0 == 0) and not st.convert
                info = []
                for d in ds:
                    wv, wk = wfetch(st, l, W_FOUT0 + hf * 8 + d)
                    bank, bkey = psF()
                    for kk in range(FH - 1 if defer else FH):
                        MM(bank[:, :Tt], wv[:, kk, :], hh_t[:, kk, :Tt], kk == 0, kk == FH - 1,
                           [wk, ("hh", kk)], [bkey])
                    info.append((d, wv, wk, bank, bkey))
                if defer:
                    for (d, wv, wk, bank, bkey) in info:
                        MM(bank[:, :Tt], wv[:, FH - 1, :], hh_t[:, FH - 1, :Tt], False, True,
                           [wk, ("hh", FH - 1)], [bkey])
                for (d, wv, wk, bank, bkey) in info:
                    TT(xres[xb][:, d, :Tt], bank[:, :Tt], xres[xb][:, d, :Tt], ALU.add,
                       [bkey, ("x", xb, d)], [("x", xb, d)])
                    if hf == 1:
                        sq_step(st, "F", d)
                    prog += FH
                yield prog / tot
        if last_layer:
            rms_from_sq(st, "F", psF)
            for k in range(KC):
                STT(xres[xb][:, k, :Tt], xres[xb][:, k, :Tt], par_t[:, L * NP + k:L * NP + k + 1], rstd[s][:, :Tt],
                    ALU.mult, ALU.mult, [("x", xb, k), ("rstd", s), ("par",)], [("x", xb, k)])
            P.dma("pool", ("ys", xb), st.ydst.rearrange("(k p) t -> p k t", p=128), xres[xb][:, :, :Tt],
                  reads=[("x", xb, k) for k in range(KC)])
            prog += 9
            yield prog / tot
        if last_layer and st.convert:
            cv_state["active"] = False
        yield 1.0

    def mk_tile(s, ti, sample=False):
        st = Stream()
        st.s = s
        st.sample = sample
        st.loaded = False
        st.next = None
        st.convert = False
        if sample:
            st.T = TS
            st.first = True
            st.last = True
            st.reset = False
            st.xsrc = xsT
            st.ydst = ysT
            st.pa_dst = lambda l: sa_o[:, l]
            st.pb_dst = lambda l: sb_o[:, l]
            st.ph_dst = lambda l: sh_o[:, l]
        else:
            st.T = T
            st.first = (ti == 0)
            st.last = (ti == NT - 1)
            st.reset = (ti == 0)
            c0 = s * SEQ + ti * T
            st.xsrc = xT[:, c0:c0 + T]
            st.ydst = yT[:, c0:c0 + T]
            st.pa_dst = lambda l: pa_o[:, l, s]
            st.pb_dst = lambda l: pb_o[:, l, s]
            st.ph_dst = lambda l: ph_o[:, l, s]
        return st

    def tasks_of(st):
        out = []
        for l in range(L):
            out.append(task_M(st, l))
            out.append(task_F(st, l, l == L - 1))
        return out

    listA = []
    listB = []
    for ti in range(NT):
        listA.append(mk_tile(0, ti))
        listB.append(mk_tile(1, ti))
    if cfg.get("SAMPLE", True):
        listA.append(mk_tile(0, 0, sample=True))

    order = []
    for ti in range(max(len(listA), len(listB))):
        if ti < len(listA):
            order.append(listA[ti])
        if ti < len(listB):
            order.append(listB[ti])
    for n, st in enumerate(order):
        st.xb = n % NXB
    listA[0].convert = True
    dummy = mk_tile(0, 0)
    dummy.xb = 0
    dummy.convert = True
    P.dry = True
    for l in range(L):
        for _ in task_M(dummy, l):
            pass
        for _ in task_F(dummy, l, l == L - 1):
            pass
    P.dry = False
    cv_state["active"] = True
    for tl in (listA, listB):
        for a, b in zip(tl[:-1], tl[1:]):
            a.next = b

    def stream_gen(tl, offset):
        idx = 0
        for st in tl:
            for l in range(L):
                for fr in task_M(st, l):
                    yield offset + idx + min(fr, 1.0)
                idx += 1
                for fr in task_F(st, l, l == L - 1):
                    yield offset + idx + min(fr, 1.0)
                idx += 1

    gA = stream_gen(listA, 0.0)
    gB = stream_gen(listB, 1.0)
    pA = next(gA, None)
    pB = next(gB, None) if False else 1.0
    startedB = False
    while pA is not None or pB is not None:
        if pB is None or (pA is not None and pA <= pB):
            pA = next(gA, None)
        else:
            pB = next(gB, None)

    P.finish("sp")
    P.emit()
    es.close()
    return nc, P


def _pack_params(inp, L):
    def chunks(v, n):
        return np.ascontiguousarray(v.reshape(n, 128).T)
    par = np.zeros((128, L * NP + 8), np.float32)
    for l in range(L):
        b = l * NP
        par[:, b + P_G1:b + P_G1 + 8] = chunks(inp["norm1_g"][l], 8)
        par[:, b + P_G2:b + P_G2 + 8] = chunks(inp["norm2_g"][l], 8)
        caw = inp["conv_a_w"][l]
        par[:, b + P_CAW:b + P_CAW + 124] = caw.T.reshape(4, 128, CAW).transpose(1, 0, 2).reshape(128, 124)
        par[:, b + P_CAB:b + P_CAB + 4] = chunks(inp["conv_a_b"][l], 4)
        par[:, b + P_LNG:b + P_LNG + 4] = chunks(inp["ln_a_g"][l], 4)
        par[:, b + P_LNB:b + P_LNB + 4] = chunks(inp["ln_a_b"][l], 4)
        cbw = inp["conv_b_w"][l]
        par[:, b + P_CBW:b + P_CBW + 16] = cbw.T.reshape(4, 128, CBW).transpose(1, 0, 2).reshape(128, 16)
        par[:, b + P_CBB:b + P_CBB + 4] = chunks(inp["conv_b_b"][l], 4)
        par[:, b + P_GRB:b + P_GRB + 4] = chunks(inp["gate_r_b"][l], 4)
        par[:, b + P_GIB:b + P_GIB + 4] = chunks(inp["gate_i_b"][l], 4)
        par[:, b + P_LAM:b + P_LAM + 4] = chunks(inp["rglru_lambda"][l], 4)
    par[:, L * NP:L * NP + 8] = chunks(inp["final_norm_g"], 8)
    return par


def _chan_in(a):
    if a.ndim == 2:
        L = a.shape[0]
        return np.ascontiguousarray(a.reshape(L, 4, 128).transpose(2, 0, 1))
    L, n, _ = a.shape
    return np.ascontiguousarray(a.reshape(L, n, 4, 128).transpose(3, 0, 2, 1))


def _chan_out(a):
    if a.ndim == 3:
        L = a.shape[1]
        return np.ascontiguousarray(a.transpose(1, 2, 0).reshape(L, 512))
    L, n = a.shape[1], a.shape[3]
    return np.ascontiguousarray(a.transpose(1, 3, 2, 0).reshape(L, n, 512))


_CACHE = {}


def run(inp, cfg, n_cores=N_CORES):
    L, NT, T, TS = cfg["L"], cfg["NT"], cfg["T"], cfg["TS"]
    SEQ = NT * T
    key = tuple(sorted(cfg.items()))
    if key not in _CACHE:
        _CACHE[key] = build(cfg)[0]
    nc = _CACHE[key]
    f32 = lambda a: np.ascontiguousarray(np.asarray(a, dtype=np.float32))
    inp = {k: f32(v) for k, v in inp.items()}
    par = _pack_params(inp, L)
    in_maps = []
    for c in range(n_cores):
        xp = inp["x_prompt"][2 * c:2 * c + 2]
        xT = np.ascontiguousarray(xp.transpose(2, 0, 1).reshape(D, 2 * SEQ))
        xsT = np.ascontiguousarray(inp["x_sample"][c].T)
        in_maps.append({
            "xT": xT, "xsT": xsT,
            "sca": _chan_in(inp["state_conv_a"][:, c]),
            "scb": _chan_in(inp["state_conv_b"][:, c]),
            "srg": _chan_in(inp["state_rglru"][:, c]),
            "par": par,
            "w_in": inp["w_in"], "w_out": inp["w_out"], "w_fin": inp["w_ffn_in"], "w_fout": inp["w_ffn_out"],
            "gate_r": inp["gate_r_w"], "gate_i": inp["gate_i_w"],
            "ident": np.eye(128, dtype=np.float32),
        })
    res = run_bass_kernel_spmd(nc, in_maps, core_ids=list(range(n_cores)))
    B = 2 * n_cores
    y_prompt = np.zeros((B, SEQ, D), np.float32)
    y_sample = np.zeros((n_cores, TS, D), np.float32)
    pa = np.zeros((L, B, CAW - 1, DC), np.float32)
    pb = np.zeros((L, B, CBW - 1, DC), np.float32)
    ph = np.zeros((L, B, DC), np.float32)
    sa = np.zeros((L, n_cores, CAW - 1, DC), np.float32)
    sbo = np.zeros((L, n_cores, CBW - 1, DC), np.float32)
    sh = np.zeros((L, n_cores, DC), np.float32)
    for c in range(n_cores):
        r = res.results[c]
        y_prompt[2 * c:2 * c + 2] = np.asarray(r["yT"]).reshape(D, 2, SEQ).transpose(1, 2, 0)
        y_sample[c] = np.asarray(r["ysT"]).T
        for s in range(2):
            pa[:, 2 * c + s] = _chan_out(np.asarray(r["pa"])[:, :, s])
            pb[:, 2 * c + s] = _chan_out(np.asarray(r["pb"])[:, :, s])
            ph[:, 2 * c + s] = _chan_out(np.asarray(r["ph"])[:, :, s])
        sa[:, c] = _chan_out(np.asarray(r["sa"]))
        sbo[:, c] = _chan_out(np.asarray(r["sb"]))
        sh[:, c] = _chan_out(np.asarray(r["sh"]))
    return (y_prompt, y_sample, pa, pb, ph, sa, sbo, sh)


FULL_CFG = {"L": 4, "NT": 8, "T": 512, "TS": 64}


def kernel(**inputs):
    return run(inputs, FULL_CFG)
```
